# Optimizing a Trainium2 kernel written in Bass

```python
import jax, jax.numpy as jnp
from jax import lax
import numpy as np

D_MODEL = 1024
BATCH = 16
SEQ = 2048
DEPTH = 4

CHUNK = 64
N_MIXERS = 3
EPS = 1e-6

GMLP_BLOCK = 128
A_WIDTH = D_MODEL
A_GROUPS = 8
A_GROUP_DIM = A_WIDTH // A_GROUPS

B_WIDTH = D_MODEL
CONV_WIDTH = 3

C_WIDTH = D_MODEL
POOL_WINDOWS = (2, 4, 8, 16)
C_GROUPS = len(POOL_WINDOWS)
C_GROUP_DIM = C_WIDTH // C_GROUPS

FFN_HIDDEN = ((8 * D_MODEL + 3 * 256 - 1) // (3 * 256)) * 256

N_A = (DEPTH + 2) // 3
N_B = (DEPTH + 1) // 3
N_C = DEPTH // 3

kernel_name = "interleaved_gmlp_shortconv_pool_trunk"


def _rms_norm(x, g):
    x32 = x.astype(jnp.float32)
    y = x32 * lax.rsqrt(jnp.mean(x32 * x32, axis=-1, keepdims=True) + EPS)
    return (y * g.astype(jnp.float32)).astype(x.dtype)


def _chunk_causal_mask(n):
    pos = jnp.arange(n)
    return (pos[None, :] // CHUNK) <= (pos[:, None] // CHUNK)


def _spatial_gating_mixer(h, w_in, v_norm_g, w_s, b_s, w_out):
    b, s, _ = h.shape
    z = jax.nn.gelu(h @ w_in, approximate=False)
    u, v = jnp.split(z, 2, axis=-1)
    v32 = v.astype(jnp.float32)
    mu = jnp.mean(v32, axis=-1, keepdims=True)
    var = jnp.mean(jnp.square(v32 - mu), axis=-1, keepdims=True)
    v = ((v32 - mu) * lax.rsqrt(var + EPS) * v_norm_g.astype(jnp.float32)).astype(h.dtype)
    v = v.reshape(b, s // GMLP_BLOCK, GMLP_BLOCK, A_GROUPS, A_GROUP_DIM)
    w_masked = jnp.where(_chunk_causal_mask(GMLP_BLOCK)[None], w_s, 0)
    sv = jnp.einsum('gij,bnjgc->bnigc', w_masked, v) + b_s.T[None, None, :, :, None]
    y = u * sv.reshape(b, s, A_WIDTH)
    return y @ w_out


def _causal_depthwise_conv(x, conv_w):
    return lax.conv_general_dilated(
        x, conv_w[:, None, :],
        window_strides=(1,),
        padding=[(CONV_WIDTH - 1, 0)],
        dimension_numbers=('NWC', 'WIO', 'NWC'),
        feature_group_count=x.shape[-1])


def _short_conv_mixer(h, w_in, conv_w, w_out):
    gate_b, gate_c, xt = jnp.split(h @ w_in, 3, axis=-1)
    y = _causal_depthwise_conv(gate_c * xt, conv_w)
    return (gate_b * y) @ w_out


def _multiscale_pool_mixer(h, w_in, w_grp, scale, w_out):
    b, s, _ = h.shape
    p = (h @ w_in).reshape(b, s, C_GROUPS, C_GROUP_DIM)
    p32 = p.astype(jnp.float32)
    cs = jnp.cumsum(p32, axis=1)
    t = jnp.arange(1, s + 1, dtype=jnp.float32)
    outs = []
    for g, w in enumerate(POOL_WINDOWS):
        csg = cs[:, :, g]
        shifted = jnp.pad(csg[:, :s - w], ((0, 0), (w, 0), (0, 0)))
        mean = (csg - shifted) / jnp.minimum(t, w)[None, :, None]
        outs.append(mean - p32[:, :, g])
    d = jnp.stack(outs, axis=2).astype(h.dtype)
    y = jnp.einsum('bsgc,gcd->bsgd', d, w_grp).reshape(b, s, C_WIDTH) * scale
    return y @ w_out


def _swiglu(h, w_gate, w_up, w_down):
    return (jax.nn.silu(h @ w_gate) * (h @ w_up)) @ w_down


def setup_inputs(seed: int = 0) -> dict:
    key = jax.random.key(seed)
    ks = jax.random.split(key, 24)
    f32 = jnp.float32

    def nrm(k, shape, scale):
        return jax.random.normal(k, shape, f32) * scale

    x = jax.random.normal(ks[0], (BATCH, SEQ, D_MODEL), f32)
    norm_mix_g = 1.0 + nrm(ks[1], (DEPTH, D_MODEL), 0.05)
    norm_ffn_g = 1.0 + nrm(ks[2], (DEPTH, D_MODEL), 0.05)
    final_norm_g = 1.0 + nrm(ks[3], (D_MODEL,), 0.05)

    a_w_in = nrm(ks[4], (N_A, D_MODEL, 2 * A_WIDTH), D_MODEL ** -0.5)
    a_v_norm_g = 1.0 + nrm(ks[5], (N_A, A_WIDTH), 0.05)
    a_w_s = nrm(ks[6], (N_A, A_GROUPS, GMLP_BLOCK, GMLP_BLOCK), GMLP_BLOCK ** -0.5)
    a_b_s = 1.0 + nrm(ks[7], (N_A, A_GROUPS, GMLP_BLOCK), 0.05)
    a_w_out = nrm(ks[8], (N_A, A_WIDTH, D_MODEL), A_WIDTH ** -0.5)

    b_w_in = nrm(ks[9], (N_B, D_MODEL, 3 * B_WIDTH), D_MODEL ** -0.5)
    b_conv_w = nrm(ks[10], (N_B, CONV_WIDTH, B_WIDTH), CONV_WIDTH ** -0.5)
    b_w_out = nrm(ks[11], (N_B, B_WIDTH, D_MODEL), B_WIDTH ** -0.5)

    c_w_in = nrm(ks[12], (N_C, D_MODEL, C_WIDTH), D_MODEL ** -0.5)
    c_w_grp = nrm(ks[13], (N_C, C_GROUPS, C_GROUP_DIM, C_GROUP_DIM), C_GROUP_DIM ** -0.5)
    c_scale = 1.0 + nrm(ks[14], (N_C, C_WIDTH), 0.1)
    c_w_out = nrm(ks[15], (N_C, C_WIDTH, D_MODEL), C_WIDTH ** -0.5)

    f_w_gate = nrm(ks[16], (DEPTH, D_MODEL, FFN_HIDDEN), D_MODEL ** -0.5)
    f_w_up = nrm(ks[17], (DEPTH, D_MODEL, FFN_HIDDEN), D_MODEL ** -0.5)
    f_w_down = nrm(ks[18], (DEPTH, FFN_HIDDEN, D_MODEL), FFN_HIDDEN ** -0.5)

    return {
        "x": x,
        "norm_mix_g": norm_mix_g, "norm_ffn_g": norm_ffn_g, "final_norm_g": final_norm_g,
        "a_w_in": a_w_in, "a_v_norm_g": a_v_norm_g, "a_w_s": a_w_s, "a_b_s": a_b_s,
        "a_w_out": a_w_out,
        "b_w_in": b_w_in, "b_conv_w": b_conv_w, "b_w_out": b_w_out,
        "c_w_in": c_w_in, "c_w_grp": c_w_grp, "c_scale": c_scale, "c_w_out": c_w_out,
        "f_w_gate": f_w_gate, "f_w_up": f_w_up, "f_w_down": f_w_down,
    }


def reference(x, norm_mix_g, norm_ffn_g, final_norm_g,
              a_w_in, a_v_norm_g, a_w_s, a_b_s, a_w_out,
              b_w_in, b_conv_w, b_w_out,
              c_w_in, c_w_grp, c_scale, c_w_out,
              f_w_gate, f_w_up, f_w_down):
    for i in range(DEPTH):
        kind = i % N_MIXERS
        j = i // N_MIXERS
        h = _rms_norm(x, norm_mix_g[i])
        if kind == 0:
            m = _spatial_gating_mixer(h, a_w_in[j], a_v_norm_g[j], a_w_s[j], a_b_s[j], a_w_out[j])
        elif kind == 1:
            m = _short_conv_mixer(h, b_w_in[j], b_conv_w[j], b_w_out[j])
        else:
            m = _multiscale_pool_mixer(h, c_w_in[j], c_w_grp[j], c_scale[j], c_w_out[j])
        x = x + m
        h = _rms_norm(x, norm_ffn_g[i])
        x = x + _swiglu(h, f_w_gate[i], f_w_up[i], f_w_down[i])
    return _rms_norm(x, final_norm_g)
```

```python
import numpy as np
from contextlib import ExitStack
import concourse.bass as bass
import concourse.mybir as mybir
from concourse.bass_utils import run_bass_kernel_spmd

F32 = mybir.dt.float32
BF16 = mybir.dt.bfloat16
AF = mybir.ActivationFunctionType
ALU = mybir.AluOpType

N_CORES = 8
D = 1024
NCH = 8
SEQ = 2048
SPC = 2
TILE = 1024
SUB = 512
NSUB = TILE // SUB
NBLK = TILE // 128
FH = 2816
FCH = FH // 128
EPS = 1e-6
POOL_WINDOWS = (2, 4, 8, 16)
DEPTH = 4
FFN_PARTS = (5, 6, 6, 5)

R_MIX, R_FFN, R_FIN, R_AVG, R_CONV, R_CSC, NVEC = 0, 32, 64, 72, 88, 112, 120

WRING_BYTES = 78 * 1024
WELEMS = WRING_BYTES // 2
N_WSEM = 6

PE, ACT, DVE, POOL, SP = "pe", "act", "dve", "pool", "sp"
SAME_ENGINE_SYNC = {"act": True, "dve": True, "pool": True}


class Op:
    __slots__ = ("eng", "fn", "deps", "signal", "count", "dma", "idx")

    def __init__(self, eng, fn, dma=None):
        self.eng = eng
        self.fn = fn
        self.deps = ()
        self.signal = False
        self.count = None
        self.dma = dma
        self.idx = -1


class Res:
    __slots__ = ("w", "r")

    def __init__(self):
        self.w = None
        self.r = {}


class Prog:
    def __init__(self):
        self.ops = []
        self.res = {}
        self.dma_cum = {}

    def add(self, eng, fn, reads=(), writes=(), dma_sem=None, group=None):
        dma = None
        if dma_sem is not None:
            cum = self.dma_cum.get(dma_sem, 0) + 16
            self.dma_cum[dma_sem] = cum
            if group is None:
                group = [cum]
            else:
                group[0] = cum
            dma = [dma_sem, group]
        op = Op(eng, fn, dma)
        op.idx = len(self.ops)
        deps = {}
        res = self.res
        for k in reads:
            r = res.get(k)
            if r is None:
                r = res[k] = Res()
            if r.w is not None:
                deps[r.w.idx] = r.w
        for k in writes:
            r = res.get(k)
            if r is None:
                r = res[k] = Res()
            if r.w is not None:
                deps[r.w.idx] = r.w
            for o in r.r.values():
                deps[o.idx] = o
        key = dma[0] if dma is not None else eng
        for k in reads:
            res[k].r[key] = op
        for k in writes:
            r = res[k]
            r.w = op
            r.r = {}
        deps.pop(op.idx, None)
        op.deps = tuple(deps.values())
        self.ops.append(op)
        return op

    @staticmethod
    def _implicit(d, op):
        return (d.dma is None and op.dma is None and d.eng == op.eng
                and (d.eng == PE or not SAME_ENGINE_SYNC.get(d.eng, True)))

    def finalize(self):
        for op in self.ops:
            for d in op.deps:
                if d.dma is not None or self._implicit(d, op):
                    continue
                d.signal = True
        cnt = {}
        for op in self.ops:
            if op.dma is None and op.signal:
                cnt[op.eng] = cnt.get(op.eng, 0) + 1
                op.count = cnt[op.eng]
        self.streams = {}
        for op in self.ops:
            self.streams.setdefault(op.eng, []).append(op)

    def emit(self, eng_name, eng, sems, final_waits=()):
        waited = {}
        for op in self.streams.get(eng_name, []):
            need = {}
            for d in op.deps:
                if d.dma is not None:
                    k, v = d.dma[0], d.dma[1][0]
                elif self._implicit(d, op):
                    continue
                else:
                    k, v = d.eng, d.count
                if v > need.get(k, 0):
                    need[k] = v
            for k, v in need.items():
                if waited.get(k, 0) >= v:
                    continue
                eng.wait_ge(sems[k], v)
                waited[k] = v
            ins = op.fn(eng)
            if op.dma is not None:
                ins.then_inc(sems[op.dma[0]], 16)
            elif op.signal:
                ins.then_inc(sems[op.eng], 1)
        for k, v in final_waits:
            if waited.get(k, 0) < v:
                eng.wait_ge(sems[k], v)
                waited[k] = v


class WBlock:
    def __init__(self, bid, off, size):
        self.id = bid
        self.off = off
        self.size = size
        self.open = True
        self.key = ("wb", bid)


class MK:
    def __init__(self, layers, final_norm, n_tiles=SPC * (SEQ // TILE)):
        self.layers = list(layers)
        self.final_norm = final_norm
        self.n_tiles = n_tiles
        self.nc = bass.Bass("TRN2", target_bir_lowering=False)
        self.P = Prog()
        self.bank_rr = 0
        self.ev_rr = 0
        self.xsq_rr = 0
        self.nrm_rr = 0
        self.io_rr = 0
        self.blocks = []
        self.live = []
        self.whead = 0
        self.store_sems = set()
        self.hook = None
        self.pending_store = None

    def run_hook(self):
        if self.hook is not None:
            f, self.hook = self.hook, None
            f()

    def pre_norm(self, pre):
        if pre is None:
            return
        self.rmsnorm(0, pre[0], pre[1])
        self.norm_sq(1)
        self.hook = lambda: self.norm_apply(1, self.norm_mm(1), pre[0], pre[1])

    def xk(self, c, tt):
        return [("x", c, tt * 4 + b) for b in range(4)]

    def bank(self):
        b = self.bank_rr
        self.bank_rr = (b + 1) % 8
        return b

    def bap(self, b):
        return self.ps[:, b * 512:(b + 1) * 512]

    def evslot(self):
        s = self.ev_rr
        self.ev_rr = (s + 1) % 3
        return s

    def walloc(self, size):
        assert size <= WELEMS
        off = 0 if (len(self.blocks) % 2 == 0) else WELEMS - size
        over = []
        keep = []
        for b in self.live:
            if b.off < off + size and off < b.off + b.size:
                assert not b.open, "weight ring too small: overlaps an open block"
                over.append(b)
            else:
                keep.append(b)
        blk = WBlock(len(self.blocks), off, size)
        self.blocks.append(blk)
        keep.append(blk)
        self.live = keep
        self.whead = off + size
        return blk, over

    def wdma(self, blk, over, pieces):
        sem = "w%d" % (blk.id % N_WSEM)
        grp = [0]
        n = len(pieces)
        for i, (dst, src) in enumerate(pieces):
            wr = []
            if i == 0:
                wr += [o.key for o in over]
            if i == n - 1:
                wr += [blk.key]
            self.P.add(POOL, lambda e, dst=dst, src=src: e.dma_start(out=dst, in_=src),
                       writes=wr, dma_sem=sem, group=grp)

    def wview(self, blk, off, n):
        return self.wring[:, blk.off + off: blk.off + off + n]

    def build(self):
        nc = self.nc
        L = self.layers
        need = lambda kind: any(l % 3 == kind for l in L)
        dr = {}
        dr["x"] = nc.dram_tensor("x", [SPC, D, SEQ], F32, kind="ExternalInput").ap()
        dr["vecs"] = nc.dram_tensor("vecs", [NVEC, 128], F32, kind="ExternalInput").ap()
        if need(0):
            dr["a_w_in"] = nc.dram_tensor("a_w_in", [2, D, 2 * D], F32, kind="ExternalInput").ap()
            dr["a_w_s"] = nc.dram_tensor("a_w_s", [2, 8, 128, 128], F32, kind="ExternalInput").ap()
            dr["a_b_s"] = nc.dram_tensor("a_b_s", [1, 2 * 8 * 128], F32, kind="ExternalInput").ap()
            dr["a_w_out"] = nc.dram_tensor("a_w_out", [2, D, D], F32, kind="ExternalInput").ap()
        if need(1):
            dr["b_w_in"] = nc.dram_tensor("b_w_in", [1, D, 3 * D], F32, kind="ExternalInput").ap()
            dr["b_w_out"] = nc.dram_tensor("b_w_out", [1, D, D], F32, kind="ExternalInput").ap()
        if need(2):
            dr["c_w_in"] = nc.dram_tensor("c_w_in", [1, D, D], F32, kind="ExternalInput").ap()
            dr["c_w_grp"] = nc.dram_tensor("c_w_grp", [1, 4, 256, 256], F32, kind="ExternalInput").ap()
            dr["c_w_out"] = nc.dram_tensor("c_w_out", [1, D, D], F32, kind="ExternalInput").ap()
        dr["f_w_gate"] = nc.dram_tensor("f_w_gate", [DEPTH, D, FH], F32, kind="ExternalInput").ap()
        dr["f_w_up"] = nc.dram_tensor("f_w_up", [DEPTH, D, FH], F32, kind="ExternalInput").ap()
        dr["f_w_down"] = nc.dram_tensor("f_w_down", [DEPTH, FH, D], F32, kind="ExternalInput").ap()
        dr["out"] = nc.dram_tensor("out", [SPC, D, SEQ], F32, kind="ExternalOutput").ap()
        self.dr = dr
        self.input_names = [k for k in dr if k != "out"]

        with ExitStack() as es:
            def sb(name, shape, dt):
                return es.enter_context(nc.sbuf_tensor(name, shape, dt))
            self.xT = sb("xT", [128, NCH, TILE], F32)
            self.hab = sb("hab", [128, 2 * NCH * TILE], BF16)
            self.h = self.hab[:, 0:NCH * TILE].rearrange("p (c t) -> p c t", c=NCH)
            self.abuf = self.hab[:, NCH * TILE:2 * NCH * TILE].rearrange("p (c t) -> p c t", c=NCH)
            self.yA = self.hab[:, NCH * TILE:2 * NCH * TILE].bitcast(F32).rearrange("p (c t) -> p c t", c=4)
            self.wring = sb("wring", [128, WELEMS], BF16)
            self.xsq = sb("xsq", [128, NCH, SUB], BF16)
            self.nrm = sb("nrm", [128, 2, SUB], F32)
            self.ev = sb("ev", [128, 3, SUB], F32)
            self.scrB = sb("scrB", [128, 8192], BF16)
            self.yB = self.scrB[:].bitcast(F32).rearrange("p (c t) -> p c t", c=4)
            self.scrF = sb("scrF", [128, 2112], F32)
            self.vecs = sb("vecsb", [128, NVEC], F32)
            self.ident = sb("ident", [128, 128], F32)
            self.ones_bf = sb("ones_bf", [128, 128], BF16)
            self.wmT = sb("wmT", [128, 16, 128], BF16)
            self.b_bc = sb("b_bc", [128, 16, 128], F32)
            self.invtab = sb("invtab", [128, 4, 16], F32)
            self.qh = sb("qh", [128, NCH, 2], F32)
            self.cstash = sb("cstash", [128, 2, 512], BF16)
            self.cm = sb("cm", [128, 12, 128], BF16)
            self.small = sb("small", [128, 64], F32)
            self.ps = es.enter_context(nc.psum_tensor("ps", [128, 8 * 512], F32))

            sem_names = [PE, ACT, DVE, POOL, "cst", "ld0", "ld1", "st0", "st1"] + ["w%d" % i for i in range(N_WSEM)]
            sems = {n: es.enter_context(nc.semaphore(n)) for n in sem_names}
            block = es.enter_context(nc.Block())

            self.record()
            P = self.P
            P.finalize()
            fin = [(s, P.dma_cum[s]) for s in ("st0", "st1") if s in P.dma_cum]

            @block.sync
            def _(e):
                P.emit(SP, e, sems, final_waits=fin)

            @block.tensor
            def _(e):
                P.emit(PE, e, sems)

            @block.scalar
            def _(e):
                P.emit(ACT, e, sems)

            @block.vector
            def _(e):
                P.emit(DVE, e, sems)

            @block.gpsimd
            def _(e):
                P.emit(POOL, e, sems)
        return nc

    def record(self):
        self.setup()
        stages = []
        for tile in range(self.n_tiles):
            stages.append(dict(w=None, c=lambda pre, nxt, tile=tile: self.load_x(tile), norm=None, out=False))
            for l in self.layers:
                kind = l % 3
                j = l // 3
                mixn = (R_MIX + l * 8, "h")
                if kind == 0:
                    stages.append(dict(w=lambda j=j: self.wblk_A(j),
                                       c=lambda wb, pre, nxt, l=l, j=j: self.mixer_A(l, j, wb, pre, nxt), norm=mixn, out=True))
                elif kind == 1:
                    for part in range(2):
                        stages.append(dict(w=lambda part=part: self.wblk_B(part),
                                           c=lambda wb, pre, nxt, l=l, part=part, tile=tile: self.mixer_B(l, part, wb, tile, pre, nxt),
                                           norm=(mixn if part == 0 else None), out=True))
                else:
                    for part in range(2):
                        stages.append(dict(w=lambda part=part: self.wblk_C(part),
                                           c=lambda wb, pre, nxt, l=l, part=part, tile=tile: self.mixer_C(l, part, wb, tile, pre, nxt),
                                           norm=(mixn if part == 0 else None), out=True))
                j0 = 0
                for pi, nj in enumerate(FFN_PARTS):
                    stages.append(dict(w=lambda l=l, j0=j0, nj=nj: self.wblk_F(l, j0, nj),
                                       c=lambda wb, pre, nxt, l=l, pi=pi, nj=nj: self.ffn_part(l, pi, nj, wb, pre, nxt),
                                       norm=((R_FFN + l * 8, "h") if pi == 0 else None), out=True))
                    j0 += nj
            stages.append(dict(w=None, c=lambda pre, nxt, tile=tile: self.store_x(tile, pre),
                               norm=((R_FIN, "y") if self.final_norm else None), out=False))
        pending = {}
        widx = [i for i, st in enumerate(stages) if st["w"] is not None]
        nxt_of = dict(zip(widx[:-1], widx[1:]))
        if widx:
            pending[widx[0]] = stages[widx[0]]["w"]()
        done_norm = False
        for i, st in enumerate(stages):
            pre = None if done_norm else st["norm"]
            nxt = None
            if st["out"] and i + 1 < len(stages):
                nxt = stages[i + 1]["norm"]
            done_norm = nxt is not None
            if st["w"] is None:
                st["c"](pre, nxt)
                continue
            n = nxt_of.get(i)
            if n is not None:
                pending[n] = stages[n]["w"]()
            wb = pending.pop(i)
            st["c"](wb, pre, nxt)
            wb["blk"].open = False

    def setup(self):
        P, dr = self.P, self.dr
        grp = [0]
        vst = self.ev[0:NVEC, 1, 0:128]
        P.add(SP, lambda e: e.dma_start(out=vst, in_=dr["vecs"][:, :]), writes=[("ev", 1)], dma_sem="cst", group=grp)
        has_a = "a_w_s" in dr
        wst = self.xT[:, 0:2, :].rearrange("p s (g j) -> p (s g) j", j=128)
        iok = [("x", c, b) for c in range(2) for b in range(NBLK)]
        if has_a:
            P.add(SP, lambda e: e.dma_start(out=wst, in_=dr["a_w_s"].rearrange("l g i j -> i (l g) j")),
                  writes=iok, dma_sem="cst", group=grp)
            P.add(SP, lambda e: e.dma_start(out=self.b_bc[:].rearrange("p g i -> p (g i)"),
                                            in_=dr["a_b_s"].partition_broadcast(128)),
                  writes=["b_bc"], dma_sem="cst", group=grp)
        onesf = self.ev[:, 0, 0:128]
        P.add(POOL, lambda e: e.memset(onesf, 1.0), writes=[("ev", 0)])
        P.add(POOL, lambda e: e.affine_select(out=self.ident[:], in_=onesf, pattern=[[-1, 128]],
                                              compare_op=ALU.is_equal, fill=0.0, base=0, channel_multiplier=1),
              reads=[("ev", 0)], writes=["ident"])
        P.add(DVE, lambda e: e.memset(self.ones_bf[:], 1.0), writes=["ones"])
        P.add(DVE, lambda e: e.memset(self.small[:, 48:49], -0.5), writes=["neghalf"])
        for g, w in enumerate(POOL_WINDOWS):
            P.add(DVE, lambda e, g=g, w=w: e.memset(self.invtab[:, g, :], 1.0 / w), writes=["invtab"])
            for t in range(w - 1):
                P.add(DVE, lambda e, g=g, t=t: e.memset(self.invtab[:, g, t:t + 1], 1.0 / (t + 1)), writes=["invtab"])
        if "c_w_in" in dr:
            band = self.ev[:, 2, 0:128]
            band2 = self.ev[:, 2, 128:256]
            tmpm = self.ev[:, 2, 256:384]
            invf = self.ev[:, 2, 384:512]
            ek = [("ev", 2)]
            for g, w in enumerate(POOL_WINDOWS):
                P.add(POOL, lambda e: e.affine_select(out=band, in_=onesf, pattern=[[1, 128]], compare_op=ALU.is_ge, fill=0.0,
                                                      base=0, channel_multiplier=-1), reads=[("ev", 0)], writes=ek)
                P.add(POOL, lambda e, w=w: e.affine_select(out=band2, in_=band, pattern=[[-1, 128]], compare_op=ALU.is_ge, fill=0.0,
                                                           base=w - 1, channel_multiplier=1), reads=ek, writes=ek)
                P.add(DVE, lambda e, g=g, w=w: e.scalar_tensor_tensor(out=self.cm[:, g * 3 + 0, :], in0=band2, scalar=1.0 / w, in1=self.ident[:],
                                                                      op0=ALU.mult, op1=ALU.subtract), reads=ek + ["ident"], writes=["cm"])
                P.add(POOL, lambda e, w=w: e.affine_select(out=tmpm, in_=onesf, pattern=[[-1, 128]], compare_op=ALU.is_ge, fill=0.0,
                                                           base=-(129 - w), channel_multiplier=1), reads=[("ev", 0)] + ek, writes=ek)
                P.add(DVE, lambda e, g=g, w=w: e.tensor_scalar(out=self.cm[:, g * 3 + 1, :], in0=tmpm, scalar1=1.0 / w, scalar2=None, op0=ALU.mult),
                      reads=ek, writes=["cm"])
                P.add(DVE, lambda e, w=w: e.memset(invf, 1.0 / w), reads=ek, writes=ek)
                P.add(DVE, lambda e, g=g: e.tensor_copy(invf[:, 0:16], self.invtab[:, g, :]), reads=ek + ["invtab"], writes=ek)
                P.add(DVE, lambda e: e.tensor_tensor(out=tmpm, in0=band2, in1=invf, op=ALU.mult), reads=ek, writes=ek)
                P.add(DVE, lambda e, g=g: e.tensor_tensor(out=self.cm[:, g * 3 + 2, :], in0=tmpm, in1=self.ident[:], op=ALU.subtract),
                      reads=ek + ["ident"], writes=["cm"])
        b = self.bank()
        P.add(PE, lambda e: e.transpose(self.bap(b)[:, 0:NVEC], vst, self.ident[0:NVEC, 0:NVEC]),
              reads=[("ev", 1), "ident"], writes=[("ps", b)])
        P.add(DVE, lambda e: e.tensor_copy(self.vecs[:], self.bap(b)[:, 0:NVEC]), reads=[("ps", b)], writes=["vecs"])
        if has_a:
            P.add(DVE, lambda e: e.memset(wst[0:64, :, 64:128], 0.0), reads=iok, writes=iok)
            for q in range(4):
                b = self.bank()
                for r in range(4):
                    lg = q * 4 + r
                    P.add(PE, lambda e, b=b, r=r, lg=lg: e.transpose(self.bap(b)[:, r * 128:(r + 1) * 128], wst[:, lg, :], self.ident[:]),
                          reads=iok + ["ident"], writes=[("ps", b)])
                P.add(ACT, lambda e, b=b, q=q: e.activation(out=self.wmT[:, q * 4:(q + 1) * 4, :],
                                                            in_=self.bap(b).rearrange("p (r i) -> p r i", r=4), func=AF.Copy),
                      reads=[("ps", b)], writes=["wmT"])
        self.scr_keys = []

    def ykeys(self, c):
        if c < 4:
            return [("a", 2 * c + i, tt) for i in range(2) for tt in range(NSUB)]
        q = c - 4
        ks = [("vn", 2 * q), ("vn", 2 * q + 1)]
        if q == 0:
            ks += [("pt", b) for b in range(4)]
        elif q == 1:
            ks += [("pt", b) for b in range(4, 8)]
        elif q == 2:
            ks += [("d", i) for i in range(4)]
        return ks

    def yview(self, c):
        return self.yA[:, c, :] if c < 4 else self.yB[:, c - 4, :]

    def load_x(self, tile):
        P, dr = self.P, self.dr
        s, half = divmod(tile, SEQ // TILE)
        self.run_hook()
        if half == 0:
            P.add(DVE, lambda e: e.memset(self.qh[:], 0.0), writes=["qh"])
        t0 = half * TILE
        for tt in range(NSUB):
            src = dr["x"][s, :, t0 + tt * SUB:t0 + (tt + 1) * SUB].rearrange("(c p) t -> p c t", p=128)
            dst = self.xT[:, :, tt * SUB:(tt + 1) * SUB]
            wk = [k for c in range(NCH) for k in self.xk(c, tt)]
            P.add(SP, lambda e, dst=dst, src=src: e.dma_start(out=dst, in_=src), writes=wk, dma_sem="ld%d" % tt)
        if self.pending_store is not None:
            f, self.pending_store = self.pending_store, None
            f()

    def store_x(self, tile, pre=None):
        P, dr = self.P, self.dr
        s, half = divmod(tile, SEQ // TILE)
        t0 = half * TILE
        self.run_hook()
        if self.final_norm:
            if pre is not None:
                for tt in range(NSUB):
                    self.rmsnorm(tt, pre[0], "y")

            def emit():
                for hh in range(2):
                    dst = dr["out"][s, hh * 512:(hh + 1) * 512, t0:t0 + TILE].rearrange("(c p) t -> p c t", p=128)
                    src = self.yA[:] if hh == 0 else self.yB[:]
                    rk = [k for c in range(hh * 4, hh * 4 + 4) for k in self.ykeys(c)]
                    P.add(SP, lambda e, dst=dst, src=src: e.dma_start(out=dst, in_=src), reads=rk,
                          writes=[("out", tile, hh)], dma_sem="st%d" % hh)
        else:
            def emit():
                for hh in range(2):
                    dst = dr["out"][s, hh * 512:(hh + 1) * 512, t0:t0 + TILE].rearrange("(c p) t -> p c t", p=128)
                    src = self.xT[:, hh * 4:(hh + 1) * 4, :]
                    rk = [("x", c, b) for c in range(hh * 4, hh * 4 + 4) for b in range(NBLK)]
                    P.add(SP, lambda e, dst=dst, src=src: e.dma_start(out=dst, in_=src), reads=rk,
                          writes=[("out", tile, hh)], dma_sem="st%d" % hh)
        if self.final_norm and tile + 1 < self.n_tiles:
            self.pending_store = emit
        else:
            emit()

    def rmsnorm(self, tt, gcol, mode="h"):
        n = self.norm_stats(tt)
        self.norm_apply(tt, n, gcol, mode)

    def norm_stats(self, tt):
        self.norm_sq(tt)
        return self.norm_mm(tt)

    def norm_sq(self, tt):
        P = self.P
        ts = slice(tt * SUB, (tt + 1) * SUB)
        for c in range(NCH):
            xin = self.xT[:, c, ts]
            xo = self.xsq[:, c, :]
            P.add(ACT, lambda e, xo=xo, xin=xin: e.activation(out=xo, in_=xin, func=AF.Square),
                  reads=self.xk(c, tt), writes=[("xsq", c)])

    def norm_mm(self, tt):
        P = self.P
        bs = self.bank()
        for c in range(NCH):
            xo = self.xsq[:, c, :]
            P.add(PE, lambda e, xo=xo, c=c: e.matmul(self.bap(bs), self.ones_bf[:], xo, start=(c == 0), stop=(c == NCH - 1)),
                  reads=[("xsq", c), "ones"], writes=[("ps", bs)])
        n = self.nrm_rr
        self.nrm_rr ^= 1
        nv = self.nrm[:, n, :]
        P.add(ACT, lambda e: e.activation(out=nv, in_=self.bap(bs), func=AF.Sqrt, scale=1.0 / D, bias=EPS),
              reads=[("ps", bs)], writes=[("nrm", n)])
        P.add(DVE, lambda e: e.reciprocal(nv, nv), reads=[("nrm", n)], writes=[("nrm", n)])
        return n

    def norm_apply(self, tt, n, gcol, mode="h"):
        P = self.P
        ts = slice(tt * SUB, (tt + 1) * SUB)
        nv = self.nrm[:, n, :]
        order = (4, 5, 6, 7, 2, 3, 0, 1) if mode == "y" else range(NCH)
        for c in order:
            xin = self.xT[:, c, ts]
            gc = self.vecs[:, gcol + c:gcol + c + 1]
            if mode == "y":
                dst, wk = self.yview(c)[:, ts], self.ykeys(c)
            else:
                dst, wk = self.h[:, c, ts], [("h", c, tt)]
            P.add(DVE, lambda e, dst=dst, xin=xin, gc=gc: e.scalar_tensor_tensor(out=dst, in0=xin, scalar=gc, in1=nv,
                                                                                 op0=ALU.mult, op1=ALU.mult),
                  reads=self.xk(c, tt) + [("nrm", n), "vecs"], writes=wk)

    def outproj(self, wout, nj, wkey, next_norm=None):
        P = self.P

        def grp_mm(tt, m):
            ts = slice(tt * SUB, (tt + 1) * SUB)
            b = self.bank()
            for j in range(nj):
                P.add(PE, lambda e, b=b, j=j, m=m, ts=ts: e.matmul(self.bap(b), wout[:, j, m * 128:(m + 1) * 128],
                                                                  self.abuf[:, j, ts], start=(j == 0), stop=(j == nj - 1)),
                      reads=[wkey, ("a", j, tt)], writes=[("ps", b)])
            return b

        def grp_upd(tt, m, b):
            xv = self.xT[:, m, tt * SUB:(tt + 1) * SUB]
            P.add(DVE, lambda e, b=b, xv=xv: e.tensor_tensor(out=xv, in0=self.bap(b), in1=xv, op=ALU.add),
                  reads=[("ps", b)] + self.xk(m, tt), writes=self.xk(m, tt))

        def grp(tt, m):
            grp_upd(tt, m, grp_mm(tt, m))

        for m in range(NCH):
            grp(0, m)
        if next_norm is None:
            for m in range(NCH):
                grp(1, m)
            return
        gcol, mode = next_norm
        for m in range(2):
            grp(1, m)
        n0 = self.norm_stats(0)
        banks = [grp_mm(1, m) for m in range(2, NCH)]
        self.norm_apply(0, n0, gcol, mode)
        for i, m in enumerate(range(2, NCH)):
            grp_upd(1, m, banks[i])
        if mode == "y":
            self.rmsnorm(1, gcol, mode)
        else:
            self.norm_sq(1)
            self.hook = lambda: self.norm_apply(1, self.norm_mm(1), gcol, mode)

    def wblk_F(self, l, j0, nj):
        dr = self.dr
        ncol = nj * 128
        size = 2 * NCH * ncol + nj * D
        blk, over = self.walloc(size)
        wg = self.wview(blk, 0, NCH * ncol).rearrange("p (k n) -> p k n", k=NCH)
        wu = self.wview(blk, NCH * ncol, NCH * ncol).rearrange("p (k n) -> p k n", k=NCH)
        wd = self.wview(blk, 2 * NCH * ncol, nj * D).rearrange("p (j n) -> p j n", j=nj)
        c0, c1 = j0 * 128, (j0 + nj) * 128
        self.wdma(blk, over, [
            (wg, dr["f_w_gate"][l, :, c0:c1].rearrange("(k p) n -> p k n", p=128)),
            (wu, dr["f_w_up"][l, :, c0:c1].rearrange("(k p) n -> p k n", p=128)),
            (wd, dr["f_w_down"][l, c0:c1, :].rearrange("(j p) n -> p j n", p=128)),
        ])
        return dict(blk=blk, wg=wg, wu=wu, wd=wd)

    def ffn_part(self, l, pi, nj, wb, pre=None, nxt=None):
        P = self.P
        wkey = wb["blk"].key
        wg, wu, wd = wb["wg"], wb["wu"], wb["wd"]
        self.pre_norm(pre)
        for tt in range(NSUB):
            ts = slice(tt * SUB, (tt + 1) * SUB)
            for j in range(nj):
                bg = self.bank()
                bu = self.bank()
                for k in range(NCH):
                    P.add(PE, lambda e, bg=bg, k=k, j=j, ts=ts: e.matmul(self.bap(bg), wg[:, k, j * 128:(j + 1) * 128], self.h[:, k, ts],
                                                                        start=(k == 0), stop=(k == NCH - 1)),
                          reads=[wkey, ("h", k, tt)], writes=[("ps", bg)])
                for k in range(NCH):
                    P.add(PE, lambda e, bu=bu, k=k, j=j, ts=ts: e.matmul(self.bap(bu), wu[:, k, j * 128:(j + 1) * 128], self.h[:, k, ts],
                                                                        start=(k == 0), stop=(k == NCH - 1)),
                          reads=[wkey, ("h", k, tt)], writes=[("ps", bu)])
                s = self.evslot()
                sg = self.ev[:, s, :]
                P.add(ACT, lambda e, sg=sg, bg=bg: e.activation(out=sg, in_=self.bap(bg), func=AF.Silu),
                      reads=[("ps", bg)], writes=[("ev", s)])
                av = self.abuf[:, j, ts]
                P.add(DVE, lambda e, av=av, bu=bu, sg=sg: e.tensor_tensor(out=av, in0=self.bap(bu), in1=sg, op=ALU.mult),
                      reads=[("ps", bu), ("ev", s)], writes=[("a", j, tt)])
                if tt == 0 and j == 1:
                    self.run_hook()
        self.outproj(wd, nj, wkey, nxt)

    def wblk_A(self, j):
        dr = self.dr
        size = NCH * 2 * D + NCH * D
        blk, over = self.walloc(size)
        win = self.wview(blk, 0, NCH * 2 * D).rearrange("p (k n) -> p k n", k=NCH)
        wout = self.wview(blk, NCH * 2 * D, NCH * D).rearrange("p (k n) -> p k n", k=NCH)
        self.wdma(blk, over, [
            (win, dr["a_w_in"][j].rearrange("(k p) n -> p k n", p=128)),
            (wout, dr["a_w_out"][j].rearrange("(k p) n -> p k n", p=128)),
        ])
        return dict(blk=blk, win=win, wout=wout)

    def mixer_A(self, l, j, wb, pre=None, nxt=None):
        P = self.P
        wkey = wb["blk"].key
        win, wout = wb["win"], wb["wout"]
        sm = self.small
        self.pre_norm(pre)
        vn = self.scrB[:].rearrange("p (b n) -> p b n", b=NBLK)
        vraw = self.scrF[:, 0:2048].rearrange("p (s n) -> p s n", s=2)
        scr = self.scr_keys
        def v_norm(blk):
            vs = blk % 2
            mean = sm[:, 24 + 2 * blk:25 + 2 * blk]
            rs = sm[:, 40 + blk:41 + blk]
            P.add(DVE, lambda e, blk=blk, vs=vs, mean=mean, rs=rs: e.tensor_scalar(out=vn[:, blk, :], in0=vraw[:, vs, :], scalar1=mean, scalar2=rs,
                                                                               op0=ALU.subtract, op1=ALU.mult),
                  reads=[("vraw", vs, 0), ("vraw", vs, 1), ("mv", blk), ("rs", blk)], writes=[("vn", blk)])

        for blk in range(NBLK):
            tt = blk // 4
            tks = slice(blk * 128, (blk + 1) * 128)
            vs = blk % 2
            for hh in range(2):
                b = self.bank()
                for k in range(NCH):
                    P.add(PE, lambda e, b=b, k=k, hh=hh, tks=tks: e.matmul(self.bap(b), self.h[:, k, tks],
                                                                          win[:, k, D + hh * 512:D + (hh + 1) * 512],
                                                                          start=(k == 0), stop=(k == NCH - 1)),
                          reads=[wkey, ("h", k, tt)], writes=[("ps", b)])
                vo = vraw[:, vs, hh * 512:(hh + 1) * 512]
                P.add(ACT, lambda e, vo=vo, b=b: e.activation(out=vo, in_=self.bap(b), func=AF.Gelu),
                      reads=[("ps", b)], writes=[("vraw", vs, hh)])
                st = sm[:, vs * 12 + hh * 6:vs * 12 + (hh + 1) * 6]
                P.add(DVE, lambda e, st=st, vo=vo: e.bn_stats(st, vo), reads=[("vraw", vs, hh)], writes=[("st", vs, hh)])
            mv = sm[:, 24 + 2 * blk:26 + 2 * blk]
            P.add(DVE, lambda e, mv=mv, vs=vs: e.bn_aggr(mv, sm[:, vs * 12:vs * 12 + 12]), reads=[("st", vs, 0), ("st", vs, 1)], writes=[("mv", blk)])
            rs = sm[:, 40 + blk:41 + blk]
            P.add(POOL, lambda e, rs=rs, blk=blk: e.tensor_scalar(out=rs, in0=sm[:, 25 + 2 * blk:26 + 2 * blk], scalar1=EPS, scalar2=1.0,
                                                                op0=ALU.add, op1=ALU.mult),
                  reads=[("mv", blk)], writes=[("rs", blk)])
            P.add(POOL, lambda e, rs=rs: e.tensor_tensor(out=rs, in0=rs, in1=sm[:, 48:49], op=ALU.pow), reads=[("rs", blk), "neghalf"], writes=[("rs", blk)])
            if blk >= 1:
                v_norm(blk - 1)
            if blk == 1:
                self.run_hook()
        v_norm(NBLK - 1)
        for tt in range(NSUB):
            ts = slice(tt * SUB, (tt + 1) * SUB)
            for g in range(NCH):
                bu = self.bank()
                for k in range(NCH):
                    P.add(PE, lambda e, bu=bu, k=k, g=g, ts=ts: e.matmul(self.bap(bu), win[:, k, g * 128:(g + 1) * 128], self.h[:, k, ts],
                                                                        start=(k == 0), stop=(k == NCH - 1)),
                          reads=[wkey, ("h", k, tt)], writes=[("ps", bu)])
                s1 = self.evslot()
                ub = self.ev[:, s1, :]
                P.add(ACT, lambda e, ub=ub, bu=bu: e.activation(out=ub, in_=self.bap(bu), func=AF.Gelu),
                      reads=[("ps", bu)], writes=[("ev", s1)])
                bsp = self.bank()
                for r in range(4):
                    blk = tt * 4 + r
                    P.add(PE, lambda e, bsp=bsp, r=r, blk=blk, g=g: e.matmul(self.bap(bsp)[:, r * 128:(r + 1) * 128],
                                                                            vn[:, blk, g * 128:(g + 1) * 128],
                                                                            self.wmT[:, j * 8 + g, :], start=True, stop=True),
                          reads=[("vn", blk), "wmT"], writes=[("ps", bsp)])
                s2 = self.evslot()
                tb = self.ev[:, s2, :]
                gam = self.vecs[:, R_AVG + j * 8 + g:R_AVG + j * 8 + g + 1]
                bb = self.b_bc[:, j * 8 + g, :].unsqueeze(1).to_broadcast([128, 4, 128])
                P.add(DVE, lambda e, tb=tb, bsp=bsp, gam=gam, bb=bb: e.scalar_tensor_tensor(
                    out=tb.rearrange("p (r i) -> p r i", r=4), in0=self.bap(bsp).rearrange("p (r i) -> p r i", r=4),
                    scalar=gam, in1=bb, op0=ALU.mult, op1=ALU.add),
                      reads=[("ps", bsp), "vecs", "b_bc"], writes=[("ev", s2)])
                av = self.abuf[:, g, ts]
                P.add(DVE, lambda e, av=av, tb=tb, ub=ub: e.tensor_tensor(out=av, in0=tb, in1=ub, op=ALU.mult),
                      reads=[("ev", s1), ("ev", s2)], writes=[("a", g, tt)])
        self.outproj(wout, NCH, wkey, nxt)

    def wblk_B(self, part):
        dr = self.dr
        nc4 = 4 * 128
        size = 3 * NCH * nc4 + 4 * D
        blk, over = self.walloc(size)
        ws = []
        pieces = []
        for i in range(3):
            v = self.wview(blk, i * NCH * nc4, NCH * nc4).rearrange("p (k n) -> p k n", k=NCH)
            ws.append(v)
            c0 = i * D + part * nc4
            pieces.append((v, dr["b_w_in"][0, :, c0:c0 + nc4].rearrange("(k p) n -> p k n", p=128)))
        wout = self.wview(blk, 3 * NCH * nc4, 4 * D).rearrange("p (j n) -> p j n", j=4)
        pieces.append((wout, dr["b_w_out"][0, part * nc4:(part + 1) * nc4, :].rearrange("(j p) n -> p j n", p=128)))
        self.wdma(blk, over, pieces)
        return dict(blk=blk, wb_=ws[0], wc=ws[1], wx=ws[2], wout=wout)

    def mixer_B(self, l, part, wb, tile, pre=None, nxt=None):
        P = self.P
        wkey = wb["blk"].key
        scr = self.scr_keys
        self.pre_norm(pre)
        qb = self.scrF[:, 0:2 * 514].rearrange("p (s n) -> p s n", s=2)
        yb = self.scrF[:, 1028:1028 + 1024].rearrange("p (s n) -> p s n", s=2)
        rr = 0
        for tt in range(NSUB):
            ts = slice(tt * SUB, (tt + 1) * SUB)
            for ci in range(4):
                c = part * 4 + ci
                banks = []
                for w in (wb["wb_"], wb["wc"], wb["wx"]):
                    b = self.bank()
                    banks.append(b)
                    for k in range(NCH):
                        P.add(PE, lambda e, b=b, k=k, w=w, ci=ci, ts=ts: e.matmul(self.bap(b), w[:, k, ci * 128:(ci + 1) * 128], self.h[:, k, ts],
                                                                                 start=(k == 0), stop=(k == NCH - 1)),
                              reads=[wkey, ("h", k, tt)], writes=[("ps", b)])
                bgb, bgc, bxt = banks
                s = self.evslot()
                xs = self.ev[:, s, :]
                P.add(ACT, lambda e, xs=xs, bxt=bxt: e.activation(out=xs, in_=self.bap(bxt), func=AF.Copy),
                      reads=[("ps", bxt)], writes=[("ev", s)])
                qs = rr
                rr ^= 1
                q = qb[:, qs, :]
                yv = yb[:, qs, :]
                hal = self.qh[:, c, :]
                P.add(DVE, lambda e, q=q, hal=hal: e.tensor_copy(q[:, 0:2], hal), reads=["qh"], writes=[("q", qs)] + scr)
                P.add(DVE, lambda e, q=q, bgc=bgc, xs=xs: e.tensor_tensor(out=q[:, 2:514], in0=self.bap(bgc), in1=xs, op=ALU.mult),
                      reads=[("ps", bgc), ("ev", s)], writes=[("q", qs)])
                P.add(DVE, lambda e, q=q, hal=hal: e.tensor_copy(hal, q[:, 512:514]), reads=[("q", qs)], writes=["qh"])
                w0 = self.vecs[:, R_CONV + 0 * 8 + c:R_CONV + 0 * 8 + c + 1]
                w1 = self.vecs[:, R_CONV + 1 * 8 + c:R_CONV + 1 * 8 + c + 1]
                w2 = self.vecs[:, R_CONV + 2 * 8 + c:R_CONV + 2 * 8 + c + 1]
                P.add(DVE, lambda e, yv=yv, q=q, w0=w0: e.tensor_scalar(out=yv, in0=q[:, 0:512], scalar1=w0, scalar2=None, op0=ALU.mult),
                      reads=[("q", qs), "vecs"], writes=[("y", qs)] + scr)
                P.add(DVE, lambda e, yv=yv, q=q, w1=w1: e.scalar_tensor_tensor(out=yv, in0=q[:, 1:513], scalar=w1, in1=yv,
                                                                              op0=ALU.mult, op1=ALU.add),
                      reads=[("q", qs), ("y", qs), "vecs"], writes=[("y", qs)])
                P.add(DVE, lambda e, yv=yv, q=q, w2=w2: e.scalar_tensor_tensor(out=yv, in0=q[:, 2:514], scalar=w2, in1=yv,
                                                                              op0=ALU.mult, op1=ALU.add),
                      reads=[("q", qs), ("y", qs), "vecs"], writes=[("y", qs)])
                av = self.abuf[:, ci, ts]
                P.add(DVE, lambda e, av=av, bgb=bgb, yv=yv: e.tensor_tensor(out=av, in0=self.bap(bgb), in1=yv, op=ALU.mult),
                      reads=[("ps", bgb), ("y", qs)], writes=[("a", ci, tt)])
                if tt == 0 and ci == 1:
                    self.run_hook()
        self.outproj(wb["wout"], 4, wkey, nxt)

    def wblk_C(self, part):
        dr = self.dr
        nc4 = 4 * 128
        size = NCH * nc4 + 4 * 256 + 4 * D
        blk, over = self.walloc(size)
        win = self.wview(blk, 0, NCH * nc4).rearrange("p (k n) -> p k n", k=NCH)
        wgr = self.wview(blk, NCH * nc4, 4 * 256).rearrange("p (g n) -> p g n", g=4)
        wout = self.wview(blk, NCH * nc4 + 4 * 256, 4 * D).rearrange("p (j n) -> p j n", j=4)
        self.wdma(blk, over, [
            (win, dr["c_w_in"][0, :, part * nc4:(part + 1) * nc4].rearrange("(k p) n -> p k n", p=128)),
            (wgr, dr["c_w_grp"][0, part * 2:part * 2 + 2].rearrange("g (b p) d -> p (g b) d", p=128)),
            (wout, dr["c_w_out"][0, part * nc4:(part + 1) * nc4, :].rearrange("(j p) n -> p j n", p=128)),
        ])
        return dict(blk=blk, win=win, wgr=wgr, wout=wout)

    def mixer_C(self, l, part, wb, tile, pre=None, nxt=None):
        P = self.P
        wkey = wb["blk"].key
        first = (tile % (SEQ // TILE) == 0)
        self.pre_norm(pre)
        ptok = self.scrB[:, 0:NBLK * 512].rearrange("p (b n) -> p b n", b=NBLK)
        db = self.scrB[:, NBLK * 512:NBLK * 512 + 4 * 512].rearrange("p (s n) -> p s n", s=4)
        win, wgr = wb["win"], wb["wgr"]
        for tt in range(NSUB):
            ts = slice(tt * SUB, (tt + 1) * SUB)
            for r in range(4):
                blk = tt * 4 + r
                tks = slice(blk * 128, (blk + 1) * 128)
                b = self.bank()
                for k in range(NCH):
                    P.add(PE, lambda e, b=b, k=k, tks=tks: e.matmul(self.bap(b), self.h[:, k, tks], win[:, k, :],
                                                                   start=(k == 0), stop=(k == NCH - 1)),
                          reads=[wkey, ("h", k, tt)], writes=[("ps", b)])
                pv = ptok[:, blk, :]
                if r % 2 == 0:
                    P.add(ACT, lambda e, pv=pv, b=b: e.activation(out=pv, in_=self.bap(b), func=AF.Copy),
                          reads=[("ps", b)], writes=[("pt", blk)])
                else:
                    P.add(DVE, lambda e, pv=pv, b=b: e.tensor_copy(pv, self.bap(b)), reads=[("ps", b)], writes=[("pt", blk)])
                if blk == 1:
                    self.run_hook()
            for ci in range(4):
                c = part * 4 + ci
                gi = ci // 2
                g = part * 2 + gi
                cs = slice(ci * 128, (ci + 1) * 128)
                b2 = self.bank()
                for r in range(4):
                    blk = tt * 4 + r
                    fb = first and blk == 0
                    m0 = self.cm[:, g * 3 + (2 if fb else 0), :]
                    o = self.bap(b2)[:, r * 128:(r + 1) * 128]
                    P.add(PE, lambda e, o=o, blk=blk, cs=cs, m0=m0, fb=fb: e.matmul(o, ptok[:, blk, cs], m0, start=True, stop=fb),
                          reads=[("pt", blk), "cm"], writes=[("ps", b2)])
                    if not fb:
                        if blk > 0:
                            pl, pk = ptok[:, blk - 1, cs], ("pt", blk - 1)
                        else:
                            pl, pk = self.cstash[:, part, cs], ("cst", part)
                        m1 = self.cm[:, g * 3 + 1, :]
                        P.add(PE, lambda e, o=o, pl=pl, m1=m1: e.matmul(o, pl, m1, start=False, stop=True),
                              reads=[pk, "cm"], writes=[("ps", b2)])
                dv = db[:, ci, :]
                P.add(DVE, lambda e, dv=dv, b2=b2: e.tensor_copy(dv, self.bap(b2)), reads=[("ps", b2)], writes=[("d", ci)])
                if ci % 2 == 1:
                    for aq in range(2):
                        co = gi * 2 + aq
                        b3 = self.bank()
                        for bq in range(2):
                            P.add(PE, lambda e, b3=b3, bq=bq, aq=aq, gi=gi: e.matmul(self.bap(b3), wgr[:, gi * 2 + bq, aq * 128:(aq + 1) * 128],
                                                                                    db[:, gi * 2 + bq, :], start=(bq == 0), stop=(bq == 1)),
                                  reads=[wkey, ("d", gi * 2 + bq)], writes=[("ps", b3)])
                        av = self.abuf[:, co, ts]
                        sc = self.vecs[:, R_CSC + part * 4 + co:R_CSC + part * 4 + co + 1]
                        P.add(ACT, lambda e, av=av, b3=b3, sc=sc: e.activation(out=av, in_=self.bap(b3), func=AF.Copy, scale=sc),
                              reads=[("ps", b3), "vecs"], writes=[("a", co, tt)])
        P.add(ACT, lambda e: e.activation(out=self.cstash[:, part, :], in_=ptok[:, NBLK - 1, :], func=AF.Copy),
              reads=[("pt", NBLK - 1)], writes=[("cst", part)])
        self.outproj(wb["wout"], 4, wkey, nxt)


_CACHE = {}


def _get_prog(layers, final_norm):
    key = (tuple(layers), final_norm)
    if key not in _CACHE:
        mk = MK(layers, final_norm)
        nc = mk.build()
        _CACHE[key] = (nc, mk.input_names)
    return _CACHE[key]


def _pack_vecs(inp):
    rows = [inp["norm_mix_g"].reshape(32, 128), inp["norm_ffn_g"].reshape(32, 128),
            inp["final_norm_g"].reshape(8, 128), inp["a_v_norm_g"].reshape(16, 128),
            inp["b_conv_w"].reshape(24, 128), inp["c_scale"].reshape(8, 128)]
    return np.ascontiguousarray(np.concatenate(rows, axis=0).astype(np.float32))


def _run(x, inp, layers, final_norm, n_cores=N_CORES):
    nc, names = _get_prog(layers, final_norm)
    shared = dict(inp)
    shared["vecs"] = _pack_vecs(inp)
    shared["a_b_s"] = np.ascontiguousarray(inp["a_b_s"].reshape(1, -1))
    in_maps = []
    for c in range(n_cores):
        m = {k: np.ascontiguousarray(shared[k]) for k in names if k != "x"}
        m["x"] = np.ascontiguousarray(x[c * SPC:(c + 1) * SPC].transpose(0, 2, 1))
        in_maps.append(m)
    res = run_bass_kernel_spmd(nc, in_maps, core_ids=list(range(n_cores)))
    return np.concatenate([np.ascontiguousarray(r["out"].transpose(0, 2, 1)) for r in res.results], axis=0)


def kernel(**inputs):
    inp = {k: np.asarray(v) for k, v in inputs.items()}
    x = np.ascontiguousarray(inp["x"], dtype=np.float32)
    out = _run(x, inp, list(range(DEPTH)), True)
    return out.astype(np.float32)
```

```python
import numpy as np
from contextlib import ExitStack
import concourse.bass as bass
import concourse.mybir as mybir
from concourse.bass_utils import run_bass_kernel_spmd

F32 = mybir.dt.float32
BF16 = mybir.dt.bfloat16
AF = mybir.ActivationFunctionType
ALU = mybir.AluOpType

N_CORES = 8
D = 1024
NCH = 8
SEQ = 2048
SPC = 2
TILE = 1024
SUB = 512
NSUB = TILE // SUB
NBLK = TILE // 128
FH = 2816
FCH = FH // 128
EPS = 1e-6
POOL_WINDOWS = (2, 4, 8, 16)
DEPTH = 4
FFN_PARTS = (5, 6, 6, 5)

R_MIX, R_FFN, R_FIN, R_AVG, R_CONV, R_CSC, NVEC = 0, 32, 64, 72, 88, 112, 120

WRING_BYTES = 78 * 1024
WELEMS = WRING_BYTES // 2
N_WSEM = 6

PE, ACT, DVE, POOL, SP = "pe", "act", "dve", "pool", "sp"
SAME_ENGINE_SYNC = {"act": True, "dve": True, "pool": True}


class Op:
    __slots__ = ("eng", "fn", "deps", "signal", "count", "dma", "idx")

    def __init__(self, eng, fn, dma=None):
        self.eng = eng
        self.fn = fn
        self.deps = ()
        self.signal = False
        self.count = None
        self.dma = dma
        self.idx = -1


class Res:
    __slots__ = ("w", "r")

    def __init__(self):
        self.w = None
        self.r = {}


class Prog:
    def __init__(self):
        self.ops = []
        self.res = {}
        self.dma_cum = {}

    def add(self, eng, fn, reads=(), writes=(), dma_sem=None, group=None):
        dma = None
        if dma_sem is not None:
            cum = self.dma_cum.get(dma_sem, 0) + 16
            self.dma_cum[dma_sem] = cum
            if group is None:
                group = [cum]
            else:
                group[0] = cum
            dma = [dma_sem, group]
        op = Op(eng, fn, dma)
        op.idx = len(self.ops)
        deps = {}
        res = self.res
        for k in reads:
            r = res.get(k)
            if r is None:
                r = res[k] = Res()
            if r.w is not None:
                deps[r.w.idx] = r.w
        for k in writes:
            r = res.get(k)
            if r is None:
                r = res[k] = Res()
            if r.w is not None:
                deps[r.w.idx] = r.w
            for o in r.r.values():
                deps[o.idx] = o
        key = dma[0] if dma is not None else eng
        for k in reads:
            res[k].r[key] = op
        for k in writes:
            r = res[k]
            r.w = op
            r.r = {}
        deps.pop(op.idx, None)
        op.deps = tuple(deps.values())
        self.ops.append(op)
        return op

    @staticmethod
    def _implicit(d, op):
        return (d.dma is None and op.dma is None and d.eng == op.eng
                and (d.eng == PE or not SAME_ENGINE_SYNC.get(d.eng, True)))

    def finalize(self):
        for op in self.ops:
            for d in op.deps:
                if d.dma is not None or self._implicit(d, op):
                    continue
                d.signal = True
        cnt = {}
        for op in self.ops:
            if op.dma is None and op.signal:
                cnt[op.eng] = cnt.get(op.eng, 0) + 1
                op.count = cnt[op.eng]
        self.streams = {}
        for op in self.ops:
            self.streams.setdefault(op.eng, []).append(op)

    def emit(self, eng_name, eng, sems, final_waits=()):
        waited = {}
        for op in self.streams.get(eng_name, []):
            need = {}
            for d in op.deps:
                if d.dma is not None:
                    k, v = d.dma[0], d.dma[1][0]
                elif self._implicit(d, op):
                    continue
                else:
                    k, v = d.eng, d.count
                if v > need.get(k, 0):
                    need[k] = v
            for k, v in need.items():
                if waited.get(k, 0) >= v:
                    continue
                eng.wait_ge(sems[k], v)
                waited[k] = v
            ins = op.fn(eng)
            if op.dma is not None:
                ins.then_inc(sems[op.dma[0]], 16)
            elif op.signal:
                ins.then_inc(sems[op.eng], 1)
        for k, v in final_waits:
            if waited.get(k, 0) < v:
                eng.wait_ge(sems[k], v)
                waited[k] = v


class WBlock:
    def __init__(self, bid, off, size):
        self.id = bid
        self.off = off
        self.size = size
        self.open = True
        self.key = ("wb", bid)


class MK:
    def __init__(self, layers, final_norm, n_tiles=SPC * (SEQ // TILE)):
        self.layers = list(layers)
        self.final_norm = final_norm
        self.n_tiles = n_tiles
        self.nc = bass.Bass("TRN2", target_bir_lowering=False)
        self.P = Prog()
        self.bank_rr = 0
        self.ev_rr = 0
        self.xsq_rr = 0
        self.nrm_rr = 0
        self.io_rr = 0
        self.blocks = []
        self.live = []
        self.whead = 0
        self.store_sems = set()
        self.hook = None
        self.pending_store = None

    def run_hook(self):
        if self.hook is not None:
            f, self.hook = self.hook, None
            f()

    def pre_norm(self, pre):
        if pre is None:
            return
        self.rmsnorm(0, pre[0], pre[1])
        self.norm_sq(1)
        self.hook = lambda: self.norm_apply(1, self.norm_mm(1), pre[0], pre[1])

    def xk(self, c, tt):
        return [("x", c, tt * 4 + b) for b in range(4)]

    def bank(self):
        b = self.bank_rr
        self.bank_rr = (b + 1) % 8
        return b

    def bap(self, b):
        return self.ps[:, b * 512:(b + 1) * 512]

    def evslot(self):
        s = self.ev_rr
        self.ev_rr = (s + 1) % 3
        return s

    def walloc(self, size):
        assert size <= WELEMS
        off = 0 if (len(self.blocks) % 2 == 0) else WELEMS - size
        over = []
        keep = []
        for b in self.live:
            if b.off < off + size and off < b.off + b.size:
                assert not b.open, "weight ring too small: overlaps an open block"
                over.append(b)
            else:
                keep.append(b)
        blk = WBlock(len(self.blocks), off, size)
        self.blocks.append(blk)
        keep.append(blk)
        self.live = keep
        self.whead = off + size
        return blk, over

    def wdma(self, blk, over, pieces):
        sem = "w%d" % (blk.id % N_WSEM)
        grp = [0]
        n = len(pieces)
        for i, (dst, src) in enumerate(pieces):
            wr = []
            if i == 0:
                wr += [o.key for o in over]
            if i == n - 1:
                wr += [blk.key]
            self.P.add(POOL, lambda e, dst=dst, src=src: e.dma_start(out=dst, in_=src),
                       writes=wr, dma_sem=sem, group=grp)

    def wview(self, blk, off, n):
        return self.wring[:, blk.off + off: blk.off + off + n]

    def build(self):
        nc = self.nc
        L = self.layers
        need = lambda kind: any(l % 3 == kind for l in L)
        dr = {}
        dr["x"] = nc.dram_tensor("x", [SPC, D, SEQ], F32, kind="ExternalInput").ap()
        dr["vecs"] = nc.dram_tensor("vecs", [NVEC, 128], F32, kind="ExternalInput").ap()
        if need(0):
            dr["a_w_in"] = nc.dram_tensor("a_w_in", [2, D, 2 * D], F32, kind="ExternalInput").ap()
            dr["a_w_s"] = nc.dram_tensor("a_w_s", [2, 8, 128, 128], F32, kind="ExternalInput").ap()
            dr["a_b_s"] = nc.dram_tensor("a_b_s", [1, 2 * 8 * 128], F32, kind="ExternalInput").ap()
            dr["a_w_out"] = nc.dram_tensor("a_w_out", [2, D, D], F32, kind="ExternalInput").ap()
        if need(1):
            dr["b_w_in"] = nc.dram_tensor("b_w_in", [1, D, 3 * D], F32, kind="ExternalInput").ap()
            dr["b_w_out"] = nc.dram_tensor("b_w_out", [1, D, D], F32, kind="ExternalInput").ap()
        if need(2):
            dr["c_w_in"] = nc.dram_tensor("c_w_in", [1, D, D], F32, kind="ExternalInput").ap()
            dr["c_w_grp"] = nc.dram_tensor("c_w_grp", [1, 4, 256, 256], F32, kind="ExternalInput").ap()
            dr["c_w_out"] = nc.dram_tensor("c_w_out", [1, D, D], F32, kind="ExternalInput").ap()
        dr["f_w_gate"] = nc.dram_tensor("f_w_gate", [DEPTH, D, FH], F32, kind="ExternalInput").ap()
        dr["f_w_up"] = nc.dram_tensor("f_w_up", [DEPTH, D, FH], F32, kind="ExternalInput").ap()
        dr["f_w_down"] = nc.dram_tensor("f_w_down", [DEPTH, FH, D], F32, kind="ExternalInput").ap()
        dr["out"] = nc.dram_tensor("out", [SPC, D, SEQ], F32, kind="ExternalOutput").ap()
        self.dr = dr
        self.input_names = [k for k in dr if k != "out"]

        with ExitStack() as es:
            def sb(name, shape, dt):
                return es.enter_context(nc.sbuf_tensor(name, shape, dt))
            self.xT = sb("xT", [128, NCH, TILE], F32)
            self.hab = sb("hab", [128, 2 * NCH * TILE], BF16)
            self.h = self.hab[:, 0:NCH * TILE].rearrange("p (c t) -> p c t", c=NCH)
            self.abuf = self.hab[:, NCH * TILE:2 * NCH * TILE].rearrange("p (c t) -> p c t", c=NCH)
            self.yA = self.hab[:, NCH * TILE:2 * NCH * TILE].bitcast(F32).rearrange("p (c t) -> p c t", c=4)
            self.wring = sb("wring", [128, WELEMS], BF16)
            self.xsq = sb("xsq", [128, NCH, SUB], BF16)
            self.nrm = sb("nrm", [128, 2, SUB], F32)
            self.ev = sb("ev", [128, 3, SUB], F32)
            self.scrB = sb("scrB", [128, 8192], BF16)
            self.yB = self.scrB[:].bitcast(F32).rearrange("p (c t) -> p c t", c=4)
            self.scrF = sb("scrF", [128, 2112], F32)
            self.vecs = sb("vecsb", [128, NVEC], F32)
            self.ident = sb("ident", [128, 128], F32)
            self.ones_bf = sb("ones_bf", [128, 128], BF16)
            self.wmT = sb("wmT", [128, 16, 128], BF16)
            self.b_bc = sb("b_bc", [128, 16, 128], F32)
            self.invtab = sb("invtab", [128, 4, 16], F32)
            self.qh = sb("qh", [128, NCH, 2], F32)
            self.cstash = sb("cstash", [128, 2, 512], BF16)
            self.cm = sb("cm", [128, 12, 128], BF16)
            self.small = sb("small", [128, 64], F32)
            self.ps = es.enter_context(nc.psum_tensor("ps", [128, 8 * 512], F32))

            sem_names = [PE, ACT, DVE, POOL, "cst", "ld0", "ld1", "st0", "st1"] + ["w%d" % i for i in range(N_WSEM)]
            sems = {n: es.enter_context(nc.semaphore(n)) for n in sem_names}
            block = es.enter_context(nc.Block())

            self.record()
            P = self.P
            P.finalize()
            fin = [(s, P.dma_cum[s]) for s in ("st0", "st1") if s in P.dma_cum]

            @block.sync
            def _(e):
                P.emit(SP, e, sems, final_waits=fin)

            @block.tensor
            def _(e):
                P.emit(PE, e, sems)

            @block.scalar
            def _(e):
                P.emit(ACT, e, sems)

            @block.vector
            def _(e):
                P.emit(DVE, e, sems)

            @block.gpsimd
            def _(e):
                P.emit(POOL, e, sems)
        return nc

    def record(self):
        self.setup()
        stages = []
        for tile in range(self.n_tiles):
            stages.append(dict(w=None, c=lambda pre, nxt, tile=tile: self.load_x(tile), norm=None, out=False))
            for l in self.layers:
                kind = l % 3
                j = l // 3
                mixn = (R_MIX + l * 8, "h")
                if kind == 0:
                    stages.append(dict(w=lambda j=j: self.wblk_A(j),
                                       c=lambda wb, pre, nxt, l=l, j=j: self.mixer_A(l, j, wb, pre, nxt), norm=mixn, out=True))
                elif kind == 1:
                    for part in range(2):
                        stages.append(dict(w=lambda part=part: self.wblk_B(part),
                                           c=lambda wb, pre, nxt, l=l, part=part, tile=tile: self.mixer_B(l, part, wb, tile, pre, nxt),
                                           norm=(mixn if part == 0 else None), out=True))
                else:
                    for part in range(2):
                        stages.append(dict(w=lambda part=part: self.wblk_C(part),
                                           c=lambda wb, pre, nxt, l=l, part=part, tile=tile: self.mixer_C(l, part, wb, tile, pre, nxt),
                                           norm=(mixn if part == 0 else None), out=True))
                j0 = 0
                for pi, nj in enumerate(FFN_PARTS):
                    stages.append(dict(w=lambda l=l, j0=j0, nj=nj: self.wblk_F(l, j0, nj),
                                       c=lambda wb, pre, nxt, l=l, pi=pi, nj=nj: self.ffn_part(l, pi, nj, wb, pre, nxt),
                                       norm=((R_FFN + l * 8, "h") if pi == 0 else None), out=True))
                    j0 += nj
            stages.append(dict(w=None, c=lambda pre, nxt, tile=tile: self.store_x(tile, pre),
                               norm=((R_FIN, "y") if self.final_norm else None), out=False))
        pending = {}
        widx = [i for i, st in enumerate(stages) if st["w"] is not None]
        nxt_of = dict(zip(widx[:-1], widx[1:]))
        if widx:
            pending[widx[0]] = stages[widx[0]]["w"]()
        done_norm = False
        for i, st in enumerate(stages):
            pre = None if done_norm else st["norm"]
            nxt = None
            if st["out"] and i + 1 < len(stages):
                nxt = stages[i + 1]["norm"]
            done_norm = nxt is not None
            if st["w"] is None:
                st["c"](pre, nxt)
                continue
            n = nxt_of.get(i)
            if n is not None:
                pending[n] = stages[n]["w"]()
            wb = pending.pop(i)
            st["c"](wb, pre, nxt)
            wb["blk"].open = False

    def setup(self):
        P, dr = self.P, self.dr
        grp = [0]
        vst = self.ev[0:NVEC, 1, 0:128]
        P.add(SP, lambda e: e.dma_start(out=vst, in_=dr["vecs"][:, :]), writes=[("ev", 1)], dma_sem="cst", group=grp)
        has_a = "a_w_s" in dr
        wst = self.xT[:, 0:2, :].rearrange("p s (g j) -> p (s g) j", j=128)
        iok = [("x", c, b) for c in range(2) for b in range(NBLK)]
        if has_a:
            P.add(SP, lambda e: e.dma_start(out=wst, in_=dr["a_w_s"].rearrange("l g i j -> i (l g) j")),
                  writes=iok, dma_sem="cst", group=grp)
            P.add(SP, lambda e: e.dma_start(out=self.b_bc[:].rearrange("p g i -> p (g i)"),
                                            in_=dr["a_b_s"].partition_broadcast(128)),
                  writes=["b_bc"], dma_sem="cst", group=grp)
        onesf = self.ev[:, 0, 0:128]
        P.add(POOL, lambda e: e.memset(onesf, 1.0), writes=[("ev", 0)])
        P.add(POOL, lambda e: e.affine_select(out=self.ident[:], in_=onesf, pattern=[[-1, 128]],
                                              compare_op=ALU.is_equal, fill=0.0, base=0, channel_multiplier=1),
              reads=[("ev", 0)], writes=["ident"])
        P.add(DVE, lambda e: e.memset(self.ones_bf[:], 1.0), writes=["ones"])
        P.add(DVE, lambda e: e.memset(self.small[:, 48:49], -0.5), writes=["neghalf"])
        for g, w in enumerate(POOL_WINDOWS):
            P.add(DVE, lambda e, g=g, w=w: e.memset(self.invtab[:, g, :], 1.0 / w), writes=["invtab"])
            for t in range(w - 1):
                P.add(DVE, lambda e, g=g, t=t: e.memset(self.invtab[:, g, t:t + 1], 1.0 / (t + 1)), writes=["invtab"])
        if "c_w_in" in dr:
            band = self.ev[:, 2, 0:128]
            band2 = self.ev[:, 2, 128:256]
            tmpm = self.ev[:, 2, 256:384]
            invf = self.ev[:, 2, 384:512]
            ek = [("ev", 2)]
            for g, w in enumerate(POOL_WINDOWS):
                P.add(POOL, lambda e: e.affine_select(out=band, in_=onesf, pattern=[[1, 128]], compare_op=ALU.is_ge, fill=0.0,
                                                      base=0, channel_multiplier=-1), reads=[("ev", 0)], writes=ek)
                P.add(POOL, lambda e, w=w: e.affine_select(out=band2, in_=band, pattern=[[-1, 128]], compare_op=ALU.is_ge, fill=0.0,
                                                           base=w - 1, channel_multiplier=1), reads=ek, writes=ek)
                P.add(DVE, lambda e, g=g, w=w: e.scalar_tensor_tensor(out=self.cm[:, g * 3 + 0, :], in0=band2, scalar=1.0 / w, in1=self.ident[:],
                                                                      op0=ALU.mult, op1=ALU.subtract), reads=ek + ["ident"], writes=["cm"])
                P.add(POOL, lambda e, w=w: e.affine_select(out=tmpm, in_=onesf, pattern=[[-1, 128]], compare_op=ALU.is_ge, fill=0.0,
                                                           base=-(129 - w), channel_multiplier=1), reads=[("ev", 0)] + ek, writes=ek)
                P.add(DVE, lambda e, g=g, w=w: e.tensor_scalar(out=self.cm[:, g * 3 + 1, :], in0=tmpm, scalar1=1.0 / w, scalar2=None, op0=ALU.mult),
                      reads=ek, writes=["cm"])
                P.add(DVE, lambda e, w=w: e.memset(invf, 1.0 / w), reads=ek, writes=ek)
                P.add(DVE, lambda e, g=g: e.tensor_copy(invf[:, 0:16], self.invtab[:, g, :]), reads=ek + ["invtab"], writes=ek)
                P.add(DVE, lambda e: e.tensor_tensor(out=tmpm, in0=band2, in1=invf, op=ALU.mult), reads=ek, writes=ek)
                P.add(DVE, lambda e, g=g: e.tensor_tensor(out=self.cm[:, g * 3 + 2, :], in0=tmpm, in1=self.ident[:], op=ALU.subtract),
                      reads=ek + ["ident"], writes=["cm"])
        b = self.bank()
        P.add(PE, lambda e: e.transpose(self.bap(b)[:, 0:NVEC], vst, self.ident[0:NVEC, 0:NVEC]),
              reads=[("ev", 1), "ident"], writes=[("ps", b)])
        P.add(DVE, lambda e: e.tensor_copy(self.vecs[:], self.bap(b)[:, 0:NVEC]), reads=[("ps", b)], writes=["vecs"])
        if has_a:
            P.add(DVE, lambda e: e.memset(wst[0:64, :, 64:128], 0.0), reads=iok, writes=iok)
            for q in range(4):
                b = self.bank()
                for r in range(4):
                    lg = q * 4 + r
                    P.add(PE, lambda e, b=b, r=r, lg=lg: e.transpose(self.bap(b)[:, r * 128:(r + 1) * 128], wst[:, lg, :], self.ident[:]),
                          reads=iok + ["ident"], writes=[("ps", b)])
                P.add(ACT, lambda e, b=b, q=q: e.activation(out=self.wmT[:, q * 4:(q + 1) * 4, :],
                                                            in_=self.bap(b).rearrange("p (r i) -> p r i", r=4), func=AF.Copy),
                      reads=[("ps", b)], writes=["wmT"])
        self.scr_keys = []

    def ykeys(self, c):
        if c < 4:
            return [("a", 2 * c + i, tt) for i in range(2) for tt in range(NSUB)]
        q = c - 4
        ks = [("vn", 2 * q), ("vn", 2 * q + 1)]
        if q == 0:
            ks += [("pt", b) for b in range(4)]
        elif q == 1:
            ks += [("pt", b) for b in range(4, 8)]
        elif q == 2:
            ks += [("d", i) for i in range(4)]
        return ks

    def yview(self, c):
        return self.yA[:, c, :] if c < 4 else self.yB[:, c - 4, :]

    def load_x(self, tile):
        P, dr = self.P, self.dr
        s, half = divmod(tile, SEQ // TILE)
        self.run_hook()
        if half == 0:
            P.add(DVE, lambda e: e.memset(self.qh[:], 0.0), writes=["qh"])
        t0 = half * TILE
        for tt in range(NSUB):
            src = dr["x"][s, :, t0 + tt * SUB:t0 + (tt + 1) * SUB].rearrange("(c p) t -> p c t", p=128)
            dst = self.xT[:, :, tt * SUB:(tt + 1) * SUB]
            wk = [k for c in range(NCH) for k in self.xk(c, tt)]
            P.add(SP, lambda e, dst=dst, src=src: e.dma_start(out=dst, in_=src), writes=wk, dma_sem="ld%d" % tt)
        if self.pending_store is not None:
            f, self.pending_store = self.pending_store, None
            f()

    def store_x(self, tile, pre=None):
        P, dr = self.P, self.dr
        s, half = divmod(tile, SEQ // TILE)
        t0 = half * TILE
        self.run_hook()
        if self.final_norm:
            if pre is not None:
                for tt in range(NSUB):
                    self.rmsnorm(tt, pre[0], "y")

            def emit():
                for hh in range(2):
                    dst = dr["out"][s, hh * 512:(hh + 1) * 512, t0:t0 + TILE].rearrange("(c p) t -> p c t", p=128)
                    src = self.yA[:] if hh == 0 else self.yB[:]
                    rk = [k for c in range(hh * 4, hh * 4 + 4) for k in self.ykeys(c)]
                    P.add(SP, lambda e, dst=dst, src=src: e.dma_start(out=dst, in_=src), reads=rk,
                          writes=[("out", tile, hh)], dma_sem="st%d" % hh)
        else:
            def emit():
                for hh in range(2):
                    dst = dr["out"][s, hh * 512:(hh + 1) * 512, t0:t0 + TILE].rearrange("(c p) t -> p c t", p=128)
                    src = self.xT[:, hh * 4:(hh + 1) * 4, :]
                    rk = [("x", c, b) for c in range(hh * 4, hh * 4 + 4) for b in range(NBLK)]
                    P.add(SP, lambda e, dst=dst, src=src: e.dma_start(out=dst, in_=src), reads=rk,
                          writes=[("out", tile, hh)], dma_sem="st%d" % hh)
        if self.final_norm and tile + 1 < self.n_tiles:
            self.pending_store = emit
        else:
            emit()

    def rmsnorm(self, tt, gcol, mode="h"):
        n = self.norm_stats(tt)
        self.norm_apply(tt, n, gcol, mode)

    def norm_stats(self, tt):
        self.norm_sq(tt)
        return self.norm_mm(tt)

    def norm_sq(self, tt):
        P = self.P
        ts = slice(tt * SUB, (tt + 1) * SUB)
        for c in range(NCH):
            xin = self.xT[:, c, ts]
            xo = self.xsq[:, c, :]
            P.add(ACT, lambda e, xo=xo, xin=xin: e.activation(out=xo, in_=xin, func=AF.Square),
                  reads=self.xk(c, tt), writes=[("xsq", c)])

    def norm_mm(self, tt):
        P = self.P
        bs = self.bank()
        for c in range(NCH):
            xo = self.xsq[:, c, :]
            P.add(PE, lambda e, xo=xo, c=c: e.matmul(self.bap(bs), self.ones_bf[:], xo, start=(c == 0), stop=(c == NCH - 1)),
                  reads=[("xsq", c), "ones"], writes=[("ps", bs)])
        n = self.nrm_rr
        self.nrm_rr ^= 1
        nv = self.nrm[:, n, :]
        P.add(ACT, lambda e: e.activation(out=nv, in_=self.bap(bs), func=AF.Ln, scale=1.0 / D, bias=EPS),
              reads=[("ps", bs)], writes=[("nrm", n)])
        P.add(ACT, lambda e: e.activation(out=nv, in_=nv, func=AF.Exp, scale=-0.5),
              reads=[("nrm", n)], writes=[("nrm", n)])
        return n

    def norm_apply(self, tt, n, gcol, mode="h"):
        P = self.P
        ts = slice(tt * SUB, (tt + 1) * SUB)
        nv = self.nrm[:, n, :]
        order = (4, 5, 6, 7, 2, 3, 0, 1) if mode == "y" else range(NCH)
        for c in order:
            xin = self.xT[:, c, ts]
            gc = self.vecs[:, gcol + c:gcol + c + 1]
            if mode == "y":
                dst, wk = self.yview(c)[:, ts], self.ykeys(c)
            else:
                dst, wk = self.h[:, c, ts], [("h", c, tt)]
            P.add(DVE, lambda e, dst=dst, xin=xin, gc=gc: e.scalar_tensor_tensor(out=dst, in0=xin, scalar=gc, in1=nv,
                                                                                 op0=ALU.mult, op1=ALU.mult),
                  reads=self.xk(c, tt) + [("nrm", n), "vecs"], writes=wk)

    def outproj(self, wout, nj, wkey, next_norm=None):
        P = self.P

        def grp_mm(tt, m):
            ts = slice(tt * SUB, (tt + 1) * SUB)
            b = self.bank()
            for j in range(nj):
                P.add(PE, lambda e, b=b, j=j, m=m, ts=ts: e.matmul(self.bap(b), wout[:, j, m * 128:(m + 1) * 128],
                                                                  self.abuf[:, j, ts], start=(j == 0), stop=(j == nj - 1)),
                      reads=[wkey, ("a", j, tt)], writes=[("ps", b)])
            return b

        def grp_upd(tt, m, b):
            xv = self.xT[:, m, tt * SUB:(tt + 1) * SUB]
            P.add(DVE, lambda e, b=b, xv=xv: e.tensor_tensor(out=xv, in0=self.bap(b), in1=xv, op=ALU.add),
                  reads=[("ps", b)] + self.xk(m, tt), writes=self.xk(m, tt))

        def grp(tt, m):
            grp_upd(tt, m, grp_mm(tt, m))

        for m in range(NCH):
            grp(0, m)
        if next_norm is None:
            for m in range(NCH):
                grp(1, m)
            return
        gcol, mode = next_norm
        for m in range(2):
            grp(1, m)
        n0 = self.norm_stats(0)
        banks = [grp_mm(1, m) for m in range(2, NCH)]
        self.norm_apply(0, n0, gcol, mode)
        for i, m in enumerate(range(2, NCH)):
            grp_upd(1, m, banks[i])
        if mode == "y":
            self.rmsnorm(1, gcol, mode)
        else:
            self.norm_sq(1)
            self.hook = lambda: self.norm_apply(1, self.norm_mm(1), gcol, mode)

    def wblk_F(self, l, j0, nj):
        dr = self.dr
        ncol = nj * 128
        size = 2 * NCH * ncol + nj * D
        blk, over = self.walloc(size)
        wg = self.wview(blk, 0, NCH * ncol).rearrange("p (k n) -> p k n", k=NCH)
        wu = self.wview(blk, NCH * ncol, NCH * ncol).rearrange("p (k n) -> p k n", k=NCH)
        wd = self.wview(blk, 2 * NCH * ncol, nj * D).rearrange("p (j n) -> p j n", j=nj)
        c0, c1 = j0 * 128, (j0 + nj) * 128
        self.wdma(blk, over, [
            (wg, dr["f_w_gate"][l, :, c0:c1].rearrange("(k p) n -> p k n", p=128)),
            (wu, dr["f_w_up"][l, :, c0:c1].rearrange("(k p) n -> p k n", p=128)),
            (wd, dr["f_w_down"][l, c0:c1, :].rearrange("(j p) n -> p j n", p=128)),
        ])
        return dict(blk=blk, wg=wg, wu=wu, wd=wd)

    def ffn_part(self, l, pi, nj, wb, pre=None, nxt=None):
        P = self.P
        wkey = wb["blk"].key
        wg, wu, wd = wb["wg"], wb["wu"], wb["wd"]
        self.pre_norm(pre)
        for tt in range(NSUB):
            ts = slice(tt * SUB, (tt + 1) * SUB)
            for j in range(nj):
                bg = self.bank()
                bu = self.bank()
                for k in range(NCH):
                    P.add(PE, lambda e, bg=bg, k=k, j=j, ts=ts: e.matmul(self.bap(bg), wg[:, k, j * 128:(j + 1) * 128], self.h[:, k, ts],
                                                                        start=(k == 0), stop=(k == NCH - 1)),
                          reads=[wkey, ("h", k, tt)], writes=[("ps", bg)])
                for k in range(NCH):
                    P.add(PE, lambda e, bu=bu, k=k, j=j, ts=ts: e.matmul(self.bap(bu), wu[:, k, j * 128:(j + 1) * 128], self.h[:, k, ts],
                                                                        start=(k == 0), stop=(k == NCH - 1)),
                          reads=[wkey, ("h", k, tt)], writes=[("ps", bu)])
                s = self.evslot()
                sg = self.ev[:, s, :]
                P.add(ACT, lambda e, sg=sg, bg=bg: e.activation(out=sg, in_=self.bap(bg), func=AF.Silu),
                      reads=[("ps", bg)], writes=[("ev", s)])
                av = self.abuf[:, j, ts]
                P.add(DVE, lambda e, av=av, bu=bu, sg=sg: e.tensor_tensor(out=av, in0=self.bap(bu), in1=sg, op=ALU.mult),
                      reads=[("ps", bu), ("ev", s)], writes=[("a", j, tt)])
                if tt == 0 and j == 1:
                    self.run_hook()
        self.outproj(wd, nj, wkey, nxt)

    def wblk_A(self, j):
        dr = self.dr
        size = NCH * 2 * D + NCH * D
        blk, over = self.walloc(size)
        win = self.wview(blk, 0, NCH * 2 * D).rearrange("p (k n) -> p k n", k=NCH)
        wout = self.wview(blk, NCH * 2 * D, NCH * D).rearrange("p (k n) -> p k n", k=NCH)
        self.wdma(blk, over, [
            (win, dr["a_w_in"][j].rearrange("(k p) n -> p k n", p=128)),
            (wout, dr["a_w_out"][j].rearrange("(k p) n -> p k n", p=128)),
        ])
        return dict(blk=blk, win=win, wout=wout)

    def mixer_A(self, l, j, wb, pre=None, nxt=None):
        P = self.P
        wkey = wb["blk"].key
        win, wout = wb["win"], wb["wout"]
        sm = self.small
        self.pre_norm(pre)
        vn = self.scrB[:].rearrange("p (b n) -> p b n", b=NBLK)
        vraw = self.scrF[:, 0:2048].rearrange("p (s n) -> p s n", s=2)
        scr = self.scr_keys
        def v_norm(blk):
            vs = blk % 2
            mean = sm[:, 24 + 2 * blk:25 + 2 * blk]
            rs = sm[:, 40 + blk:41 + blk]
            P.add(DVE, lambda e, blk=blk, vs=vs, mean=mean, rs=rs: e.tensor_scalar(out=vn[:, blk, :], in0=vraw[:, vs, :], scalar1=mean, scalar2=rs,
                                                                               op0=ALU.subtract, op1=ALU.mult),
                  reads=[("vraw", vs, 0), ("vraw", vs, 1), ("mv", blk), ("rs", blk)], writes=[("vn", blk)])

        for blk in range(NBLK):
            tt = blk // 4
            tks = slice(blk * 128, (blk + 1) * 128)
            vs = blk % 2
            for hh in range(2):
                b = self.bank()
                for k in range(NCH):
                    P.add(PE, lambda e, b=b, k=k, hh=hh, tks=tks: e.matmul(self.bap(b), self.h[:, k, tks],
                                                                          win[:, k, D + hh * 512:D + (hh + 1) * 512],
                                                                          start=(k == 0), stop=(k == NCH - 1)),
                          reads=[wkey, ("h", k, tt)], writes=[("ps", b)])
                vo = vraw[:, vs, hh * 512:(hh + 1) * 512]
                P.add(ACT, lambda e, vo=vo, b=b: e.activation(out=vo, in_=self.bap(b), func=AF.Gelu),
                      reads=[("ps", b)], writes=[("vraw", vs, hh)])
                st = sm[:, vs * 12 + hh * 6:vs * 12 + (hh + 1) * 6]
                P.add(DVE, lambda e, st=st, vo=vo: e.bn_stats(st, vo), reads=[("vraw", vs, hh)], writes=[("st", vs, hh)])
            mv = sm[:, 24 + 2 * blk:26 + 2 * blk]
            P.add(DVE, lambda e, mv=mv, vs=vs: e.bn_aggr(mv, sm[:, vs * 12:vs * 12 + 12]), reads=[("st", vs, 0), ("st", vs, 1)], writes=[("mv", blk)])
            rs = sm[:, 40 + blk:41 + blk]
            P.add(POOL, lambda e, rs=rs, blk=blk: e.tensor_scalar(out=rs, in0=sm[:, 25 + 2 * blk:26 + 2 * blk], scalar1=EPS, scalar2=1.0,
                                                                op0=ALU.add, op1=ALU.mult),
                  reads=[("mv", blk)], writes=[("rs", blk)])
            P.add(POOL, lambda e, rs=rs: e.tensor_tensor(out=rs, in0=rs, in1=sm[:, 48:49], op=ALU.pow), reads=[("rs", blk), "neghalf"], writes=[("rs", blk)])
            if blk >= 1:
                v_norm(blk - 1)
            if blk == 1:
                self.run_hook()
        v_norm(NBLK - 1)
        for tt in range(NSUB):
            ts = slice(tt * SUB, (tt + 1) * SUB)
            for g in range(NCH):
                bu = self.bank()
                for k in range(NCH):
                    P.add(PE, lambda e, bu=bu, k=k, g=g, ts=ts: e.matmul(self.bap(bu), win[:, k, g * 128:(g + 1) * 128], self.h[:, k, ts],
                                                                        start=(k == 0), stop=(k == NCH - 1)),
                          reads=[wkey, ("h", k, tt)], writes=[("ps", bu)])
                s1 = self.evslot()
                ub = self.ev[:, s1, :]
                P.add(ACT, lambda e, ub=ub, bu=bu: e.activation(out=ub, in_=self.bap(bu), func=AF.Gelu),
                      reads=[("ps", bu)], writes=[("ev", s1)])
                bsp = self.bank()
                for r in range(4):
                    blk = tt * 4 + r
                    P.add(PE, lambda e, bsp=bsp, r=r, blk=blk, g=g: e.matmul(self.bap(bsp)[:, r * 128:(r + 1) * 128],
                                                                            vn[:, blk, g * 128:(g + 1) * 128],
                                                                            self.wmT[:, j * 8 + g, :], start=True, stop=True),
                          reads=[("vn", blk), "wmT"], writes=[("ps", bsp)])
                s2 = self.evslot()
                tb = self.ev[:, s2, :]
                gam = self.vecs[:, R_AVG + j * 8 + g:R_AVG + j * 8 + g + 1]
                bb = self.b_bc[:, j * 8 + g, :].unsqueeze(1).to_broadcast([128, 4, 128])
                P.add(DVE, lambda e, tb=tb, bsp=bsp, gam=gam, bb=bb: e.scalar_tensor_tensor(
                    out=tb.rearrange("p (r i) -> p r i", r=4), in0=self.bap(bsp).rearrange("p (r i) -> p r i", r=4),
                    scalar=gam, in1=bb, op0=ALU.mult, op1=ALU.add),
                      reads=[("ps", bsp), "vecs", "b_bc"], writes=[("ev", s2)])
                av = self.abuf[:, g, ts]
                P.add(DVE, lambda e, av=av, tb=tb, ub=ub: e.tensor_tensor(out=av, in0=tb, in1=ub, op=ALU.mult),
                      reads=[("ev", s1), ("ev", s2)], writes=[("a", g, tt)])
        self.outproj(wout, NCH, wkey, nxt)

    def wblk_B(self, part):
        dr = self.dr
        nc4 = 4 * 128
        size = 3 * NCH * nc4 + 4 * D
        blk, over = self.walloc(size)
        ws = []
        pieces = []
        for i in range(3):
            v = self.wview(blk, i * NCH * nc4, NCH * nc4).rearrange("p (k n) -> p k n", k=NCH)
            ws.append(v)
            c0 = i * D + part * nc4
            pieces.append((v, dr["b_w_in"][0, :, c0:c0 + nc4].rearrange("(k p) n -> p k n", p=128)))
        wout = self.wview(blk, 3 * NCH * nc4, 4 * D).rearrange("p (j n) -> p j n", j=4)
        pieces.append((wout, dr["b_w_out"][0, part * nc4:(part + 1) * nc4, :].rearrange("(j p) n -> p j n", p=128)))
        self.wdma(blk, over, pieces)
        return dict(blk=blk, wb_=ws[0], wc=ws[1], wx=ws[2], wout=wout)

    def mixer_B(self, l, part, wb, tile, pre=None, nxt=None):
        P = self.P
        wkey = wb["blk"].key
        scr = self.scr_keys
        self.pre_norm(pre)
        qb = self.scrF[:, 0:2 * 514].rearrange("p (s n) -> p s n", s=2)
        yb = self.scrF[:, 1028:1028 + 1024].rearrange("p (s n) -> p s n", s=2)
        rr = 0
        for tt in range(NSUB):
            ts = slice(tt * SUB, (tt + 1) * SUB)
            for ci in range(4):
                c = part * 4 + ci
                banks = []
                for w in (wb["wb_"], wb["wc"], wb["wx"]):
                    b = self.bank()
                    banks.append(b)
                    for k in range(NCH):
                        P.add(PE, lambda e, b=b, k=k, w=w, ci=ci, ts=ts: e.matmul(self.bap(b), w[:, k, ci * 128:(ci + 1) * 128], self.h[:, k, ts],
                                                                                 start=(k == 0), stop=(k == NCH - 1)),
                              reads=[wkey, ("h", k, tt)], writes=[("ps", b)])
                bgb, bgc, bxt = banks
                s = self.evslot()
                xs = self.ev[:, s, :]
                P.add(ACT, lambda e, xs=xs, bxt=bxt: e.activation(out=xs, in_=self.bap(bxt), func=AF.Copy),
                      reads=[("ps", bxt)], writes=[("ev", s)])
                qs = rr
                rr ^= 1
                q = qb[:, qs, :]
                yv = yb[:, qs, :]
                hal = self.qh[:, c, :]
                P.add(DVE, lambda e, q=q, hal=hal: e.tensor_copy(q[:, 0:2], hal), reads=["qh"], writes=[("q", qs)] + scr)
                P.add(DVE, lambda e, q=q, bgc=bgc, xs=xs: e.tensor_tensor(out=q[:, 2:514], in0=self.bap(bgc), in1=xs, op=ALU.mult),
                      reads=[("ps", bgc), ("ev", s)], writes=[("q", qs)])
                P.add(DVE, lambda e, q=q, hal=hal: e.tensor_copy(hal, q[:, 512:514]), reads=[("q", qs)], writes=["qh"])
                w0 = self.vecs[:, R_CONV + 0 * 8 + c:R_CONV + 0 * 8 + c + 1]
                w1 = self.vecs[:, R_CONV + 1 * 8 + c:R_CONV + 1 * 8 + c + 1]
                w2 = self.vecs[:, R_CONV + 2 * 8 + c:R_CONV + 2 * 8 + c + 1]
                P.add(DVE, lambda e, yv=yv, q=q, w0=w0: e.tensor_scalar(out=yv, in0=q[:, 0:512], scalar1=w0, scalar2=None, op0=ALU.mult),
                      reads=[("q", qs), "vecs"], writes=[("y", qs)] + scr)
                P.add(DVE, lambda e, yv=yv, q=q, w1=w1: e.scalar_tensor_tensor(out=yv, in0=q[:, 1:513], scalar=w1, in1=yv,
                                                                              op0=ALU.mult, op1=ALU.add),
                      reads=[("q", qs), ("y", qs), "vecs"], writes=[("y", qs)])
                P.add(DVE, lambda e, yv=yv, q=q, w2=w2: e.scalar_tensor_tensor(out=yv, in0=q[:, 2:514], scalar=w2, in1=yv,
                                                                              op0=ALU.mult, op1=ALU.add),
                      reads=[("q", qs), ("y", qs), "vecs"], writes=[("y", qs)])
                av = self.abuf[:, ci, ts]
                P.add(DVE, lambda e, av=av, bgb=bgb, yv=yv: e.tensor_tensor(out=av, in0=self.bap(bgb), in1=yv, op=ALU.mult),
                      reads=[("ps", bgb), ("y", qs)], writes=[("a", ci, tt)])
                if tt == 0 and ci == 1:
                    self.run_hook()
        self.outproj(wb["wout"], 4, wkey, nxt)

    def wblk_C(self, part):
        dr = self.dr
        nc4 = 4 * 128
        size = NCH * nc4 + 4 * 256 + 4 * D
        blk, over = self.walloc(size)
        win = self.wview(blk, 0, NCH * nc4).rearrange("p (k n) -> p k n", k=NCH)
        wgr = self.wview(blk, NCH * nc4, 4 * 256).rearrange("p (g n) -> p g n", g=4)
        wout = self.wview(blk, NCH * nc4 + 4 * 256, 4 * D).rearrange("p (j n) -> p j n", j=4)
        self.wdma(blk, over, [
            (win, dr["c_w_in"][0, :, part * nc4:(part + 1) * nc4].rearrange("(k p) n -> p k n", p=128)),
            (wgr, dr["c_w_grp"][0, part * 2:part * 2 + 2].rearrange("g (b p) d -> p (g b) d", p=128)),
            (wout, dr["c_w_out"][0, part * nc4:(part + 1) * nc4, :].rearrange("(j p) n -> p j n", p=128)),
        ])
        return dict(blk=blk, win=win, wgr=wgr, wout=wout)

    def mixer_C(self, l, part, wb, tile, pre=None, nxt=None):
        P = self.P
        wkey = wb["blk"].key
        first = (tile % (SEQ // TILE) == 0)
        self.pre_norm(pre)
        ptok = self.scrB[:, 0:NBLK * 512].rearrange("p (b n) -> p b n", b=NBLK)
        db = self.scrB[:, NBLK * 512:NBLK * 512 + 4 * 512].rearrange("p (s n) -> p s n", s=4)
        win, wgr = wb["win"], wb["wgr"]
        for tt in range(NSUB):
            ts = slice(tt * SUB, (tt + 1) * SUB)
            for r in range(4):
                blk = tt * 4 + r
                tks = slice(blk * 128, (blk + 1) * 128)
                b = self.bank()
                for k in range(NCH):
                    P.add(PE, lambda e, b=b, k=k, tks=tks: e.matmul(self.bap(b), self.h[:, k, tks], win[:, k, :],
                                                                   start=(k == 0), stop=(k == NCH - 1)),
                          reads=[wkey, ("h", k, tt)], writes=[("ps", b)])
                pv = ptok[:, blk, :]
                if r % 2 == 0:
                    P.add(ACT, lambda e, pv=pv, b=b: e.activation(out=pv, in_=self.bap(b), func=AF.Copy),
                          reads=[("ps", b)], writes=[("pt", blk)])
                else:
                    P.add(DVE, lambda e, pv=pv, b=b: e.tensor_copy(pv, self.bap(b)), reads=[("ps", b)], writes=[("pt", blk)])
                if blk == 1:
                    self.run_hook()
            for ci in range(4):
                c = part * 4 + ci
                gi = ci // 2
                g = part * 2 + gi
                cs = slice(ci * 128, (ci + 1) * 128)
                b2 = self.bank()
                for r in range(4):
                    blk = tt * 4 + r
                    fb = first and blk == 0
                    m0 = self.cm[:, g * 3 + (2 if fb else 0), :]
                    o = self.bap(b2)[:, r * 128:(r + 1) * 128]
                    P.add(PE, lambda e, o=o, blk=blk, cs=cs, m0=m0, fb=fb: e.matmul(o, ptok[:, blk, cs], m0, start=True, stop=fb),
                          reads=[("pt", blk), "cm"], writes=[("ps", b2)])
                    if not fb:
                        if blk > 0:
                            pl, pk = ptok[:, blk - 1, cs], ("pt", blk - 1)
                        else:
                            pl, pk = self.cstash[:, part, cs], ("cst", part)
                        m1 = self.cm[:, g * 3 + 1, :]
                        P.add(PE, lambda e, o=o, pl=pl, m1=m1: e.matmul(o, pl, m1, start=False, stop=True),
                              reads=[pk, "cm"], writes=[("ps", b2)])
                dv = db[:, ci, :]
                P.add(DVE, lambda e, dv=dv, b2=b2: e.tensor_copy(dv, self.bap(b2)), reads=[("ps", b2)], writes=[("d", ci)])
                if ci % 2 == 1:
                    for aq in range(2):
                        co = gi * 2 + aq
                        b3 = self.bank()
                        for bq in range(2):
                            P.add(PE, lambda e, b3=b3, bq=bq, aq=aq, gi=gi: e.matmul(self.bap(b3), wgr[:, gi * 2 + bq, aq * 128:(aq + 1) * 128],
                                                                                    db[:, gi * 2 + bq, :], start=(bq == 0), stop=(bq == 1)),
                                  reads=[wkey, ("d", gi * 2 + bq)], writes=[("ps", b3)])
                        av = self.abuf[:, co, ts]
                        sc = self.vecs[:, R_CSC + part * 4 + co:R_CSC + part * 4 + co + 1]
                        P.add(ACT, lambda e, av=av, b3=b3, sc=sc: e.activation(out=av, in_=self.bap(b3), func=AF.Copy, scale=sc),
                              reads=[("ps", b3), "vecs"], writes=[("a", co, tt)])
        P.add(ACT, lambda e: e.activation(out=self.cstash[:, part, :], in_=ptok[:, NBLK - 1, :], func=AF.Copy),
              reads=[("pt", NBLK - 1)], writes=[("cst", part)])
        self.outproj(wb["wout"], 4, wkey, nxt)


_CACHE = {}


def _get_prog(layers, final_norm):
    key = (tuple(layers), final_norm)
    if key not in _CACHE:
        mk = MK(layers, final_norm)
        nc = mk.build()
        _CACHE[key] = (nc, mk.input_names)
    return _CACHE[key]


def _pack_vecs(inp):
    rows = [inp["norm_mix_g"].reshape(32, 128), inp["norm_ffn_g"].reshape(32, 128),
            inp["final_norm_g"].reshape(8, 128), inp["a_v_norm_g"].reshape(16, 128),
            inp["b_conv_w"].reshape(24, 128), inp["c_scale"].reshape(8, 128)]
    return np.ascontiguousarray(np.concatenate(rows, axis=0).astype(np.float32))


def _run(x, inp, layers, final_norm, n_cores=N_CORES):
    nc, names = _get_prog(layers, final_norm)
    shared = dict(inp)
    shared["vecs"] = _pack_vecs(inp)
    shared["a_b_s"] = np.ascontiguousarray(inp["a_b_s"].reshape(1, -1))
    in_maps = []
    for c in range(n_cores):
        m = {k: np.ascontiguousarray(shared[k]) for k in names if k != "x"}
        m["x"] = np.ascontiguousarray(x[c * SPC:(c + 1) * SPC].transpose(0, 2, 1))
        in_maps.append(m)
    res = run_bass_kernel_spmd(nc, in_maps, core_ids=list(range(n_cores)))
    return np.concatenate([np.ascontiguousarray(r["out"].transpose(0, 2, 1)) for r in res.results], axis=0)


def kernel(**inputs):
    inp = {k: np.asarray(v) for k, v in inputs.items()}
    x = np.ascontiguousarray(inp["x"], dtype=np.float32)
    out = _run(x, inp, list(range(DEPTH)), True)
    return out.astype(np.float32)
```

```python
import numpy as np
from contextlib import ExitStack
import concourse.bass as bass
import concourse.mybir as mybir
from concourse.bass_utils import run_bass_kernel_spmd

F32 = mybir.dt.float32
BF16 = mybir.dt.bfloat16
AF = mybir.ActivationFunctionType
ALU = mybir.AluOpType

N_CORES = 8
D = 1024
NCH = 8
SEQ = 2048
SPC = 2
TILE = 1024
SUB = 512
NSUB = TILE // SUB
NBLK = TILE // 128
FH = 2816
FCH = FH // 128
EPS = 1e-6
POOL_WINDOWS = (2, 4, 8, 16)
DEPTH = 4
FFN_PARTS = (5, 6, 6, 5)

R_MIX, R_FFN, R_FIN, R_AVG, R_CONV, R_CSC, NVEC = 0, 32, 64, 72, 88, 112, 120

WRING_BYTES = 78 * 1024
WELEMS = WRING_BYTES // 2
N_WSEM = 12

PE, ACT, DVE, POOL, SP = "pe", "act", "dve", "pool", "sp"
SAME_ENGINE_SYNC = {"act": True, "dve": True, "pool": True}


class Op:
    __slots__ = ("eng", "fn", "deps", "signal", "count", "dma", "idx")

    def __init__(self, eng, fn, dma=None):
        self.eng = eng
        self.fn = fn
        self.deps = ()
        self.signal = False
        self.count = None
        self.dma = dma
        self.idx = -1


class Res:
    __slots__ = ("w", "r")

    def __init__(self):
        self.w = None
        self.r = {}


class Prog:
    def __init__(self):
        self.ops = []
        self.res = {}
        self.dma_cum = {}

    def add(self, eng, fn, reads=(), writes=(), dma_sem=None, group=None):
        dma = None
        if dma_sem is not None:
            cum = self.dma_cum.get(dma_sem, 0) + 16
            self.dma_cum[dma_sem] = cum
            if group is None:
                group = [cum]
            else:
                group[0] = cum
            dma = [dma_sem, group]
        op = Op(eng, fn, dma)
        op.idx = len(self.ops)
        deps = {}
        res = self.res
        for k in reads:
            r = res.get(k)
            if r is None:
                r = res[k] = Res()
            if r.w is not None:
                deps[r.w.idx] = r.w
        for k in writes:
            r = res.get(k)
            if r is None:
                r = res[k] = Res()
            if r.w is not None:
                deps[r.w.idx] = r.w
            for o in r.r.values():
                deps[o.idx] = o
        key = dma[0] if dma is not None else eng
        for k in reads:
            res[k].r[key] = op
        for k in writes:
            r = res[k]
            r.w = op
            r.r = {}
        deps.pop(op.idx, None)
        op.deps = tuple(deps.values())
        self.ops.append(op)
        return op

    @staticmethod
    def _implicit(d, op):
        return (d.dma is None and op.dma is None and d.eng == op.eng
                and (d.eng == PE or not SAME_ENGINE_SYNC.get(d.eng, True)))

    def finalize(self):
        for op in self.ops:
            for d in op.deps:
                if d.dma is not None or self._implicit(d, op):
                    continue
                d.signal = True
        cnt = {}
        for op in self.ops:
            if op.dma is None and op.signal:
                cnt[op.eng] = cnt.get(op.eng, 0) + 1
                op.count = cnt[op.eng]
        self.streams = {}
        for op in self.ops:
            self.streams.setdefault(op.eng, []).append(op)

    def emit(self, eng_name, eng, sems, final_waits=()):
        waited = {}
        for op in self.streams.get(eng_name, []):
            need = {}
            for d in op.deps:
                if d.dma is not None:
                    k, v = d.dma[0], d.dma[1][0]
                elif self._implicit(d, op):
                    continue
                else:
                    k, v = d.eng, d.count
                if v > need.get(k, 0):
                    need[k] = v
            for k, v in need.items():
                if waited.get(k, 0) >= v:
                    continue
                eng.wait_ge(sems[k], v)
                waited[k] = v
            ins = op.fn(eng)
            if op.dma is not None:
                ins.then_inc(sems[op.dma[0]], 16)
            elif op.signal:
                ins.then_inc(sems[op.eng], 1)
        for k, v in final_waits:
            if waited.get(k, 0) < v:
                eng.wait_ge(sems[k], v)
                waited[k] = v


class WBlock:
    def __init__(self, bid, off, size):
        self.id = bid
        self.off = off
        self.size = size
        self.open = True
        self.keys = []


class MK:
    def __init__(self, layers, final_norm, n_tiles=SPC * (SEQ // TILE)):
        self.layers = list(layers)
        self.final_norm = final_norm
        self.n_tiles = n_tiles
        self.nc = bass.Bass("TRN2", target_bir_lowering=False)
        self.P = Prog()
        self.bank_rr = 0
        self.ev_rr = 0
        self.xsq_rr = 0
        self.nrm_rr = 0
        self.io_rr = 0
        self.blocks = []
        self.live = []
        self.whead = 0
        self.store_sems = set()
        self.hook = None
        self.pending_store = None
        self.wsem_rr = 0

    def run_hook(self):
        if self.hook is not None:
            f, self.hook = self.hook, None
            f()

    def pre_norm(self, pre):
        if pre is None:
            return
        self.rmsnorm(0, pre[0], pre[1])
        self.norm_sq(1)
        self.hook = lambda: self.norm_apply(1, self.norm_mm(1), pre[0], pre[1])

    def xk(self, c, tt):
        return [("x", c, tt * 4 + b) for b in range(4)]

    def bank(self):
        b = self.bank_rr
        self.bank_rr = (b + 1) % 8
        return b

    def bap(self, b):
        return self.ps[:, b * 512:(b + 1) * 512]

    def evslot(self):
        s = self.ev_rr
        self.ev_rr = (s + 1) % 3
        return s

    def walloc(self, size):
        assert size <= WELEMS
        off = 0 if (len(self.blocks) % 2 == 0) else WELEMS - size
        over = []
        keep = []
        for b in self.live:
            if b.off < off + size and off < b.off + b.size:
                assert not b.open, "weight ring too small: overlaps an open block"
                over.append(b)
            else:
                keep.append(b)
        blk = WBlock(len(self.blocks), off, size)
        self.blocks.append(blk)
        keep.append(blk)
        self.live = keep
        self.whead = off + size
        return blk, over

    def wdma(self, blk, over, pieces):
        for i, (dst, src, name) in enumerate(pieces):
            sem = "w%d" % (self.wsem_rr % N_WSEM)
            self.wsem_rr += 1
            key = ("wb", blk.id, name)
            blk.keys.append(key)
            wr = [key]
            if i == 0:
                wr += [k for o in over for k in o.keys]
            self.P.add(POOL, lambda e, dst=dst, src=src: e.dma_start(out=dst, in_=src), writes=wr, dma_sem=sem)
        return {name: ("wb", blk.id, name) for (_, _, name) in pieces}

    def wview(self, blk, off, n):
        return self.wring[:, blk.off + off: blk.off + off + n]

    def build(self):
        nc = self.nc
        L = self.layers
        need = lambda kind: any(l % 3 == kind for l in L)
        dr = {}
        dr["x"] = nc.dram_tensor("x", [SPC, D, SEQ], F32, kind="ExternalInput").ap()
        dr["vecs"] = nc.dram_tensor("vecs", [NVEC, 128], F32, kind="ExternalInput").ap()
        if need(0):
            dr["a_w_in"] = nc.dram_tensor("a_w_in", [2, D, 2 * D], F32, kind="ExternalInput").ap()
            dr["a_w_s"] = nc.dram_tensor("a_w_s", [2, 8, 128, 128], F32, kind="ExternalInput").ap()
            dr["a_b_s"] = nc.dram_tensor("a_b_s", [1, 2 * 8 * 128], F32, kind="ExternalInput").ap()
            dr["a_w_out"] = nc.dram_tensor("a_w_out", [2, D, D], F32, kind="ExternalInput").ap()
        if need(1):
            dr["b_w_in"] = nc.dram_tensor("b_w_in", [1, D, 3 * D], F32, kind="ExternalInput").ap()
            dr["b_w_out"] = nc.dram_tensor("b_w_out", [1, D, D], F32, kind="ExternalInput").ap()
        if need(2):
            dr["c_w_in"] = nc.dram_tensor("c_w_in", [1, D, D], F32, kind="ExternalInput").ap()
            dr["c_w_grp"] = nc.dram_tensor("c_w_grp", [1, 4, 256, 256], F32, kind="ExternalInput").ap()
            dr["c_w_out"] = nc.dram_tensor("c_w_out", [1, D, D], F32, kind="ExternalInput").ap()
        dr["f_w_gate"] = nc.dram_tensor("f_w_gate", [DEPTH, D, FH], F32, kind="ExternalInput").ap()
        dr["f_w_up"] = nc.dram_tensor("f_w_up", [DEPTH, D, FH], F32, kind="ExternalInput").ap()
        dr["f_w_down"] = nc.dram_tensor("f_w_down", [DEPTH, FH, D], F32, kind="ExternalInput").ap()
        dr["out"] = nc.dram_tensor("out", [SPC, D, SEQ], F32, kind="ExternalOutput").ap()
        self.dr = dr
        self.input_names = [k for k in dr if k != "out"]

        with ExitStack() as es:
            def sb(name, shape, dt):
                return es.enter_context(nc.sbuf_tensor(name, shape, dt))
            self.xT = sb("xT", [128, NCH, TILE], F32)
            self.hab = sb("hab", [128, 2 * NCH * TILE], BF16)
            self.h = self.hab[:, 0:NCH * TILE].rearrange("p (c t) -> p c t", c=NCH)
            self.abuf = self.hab[:, NCH * TILE:2 * NCH * TILE].rearrange("p (c t) -> p c t", c=NCH)
            self.yA = self.hab[:, NCH * TILE:2 * NCH * TILE].bitcast(F32).rearrange("p (c t) -> p c t", c=4)
            self.wring = sb("wring", [128, WELEMS], BF16)
            self.xsq = sb("xsq", [128, NCH, SUB], BF16)
            self.nrm = sb("nrm", [128, 2, SUB], F32)
            self.ev = sb("ev", [128, 3, SUB], F32)
            self.scrB = sb("scrB", [128, 8192], BF16)
            self.yB = self.scrB[:].bitcast(F32).rearrange("p (c t) -> p c t", c=4)
            self.scrF = sb("scrF", [128, 2112], F32)
            self.vecs = sb("vecsb", [128, NVEC], F32)
            self.ident = sb("ident", [128, 128], F32)
            self.ones_bf = sb("ones_bf", [128, 128], BF16)
            self.wmT = sb("wmT", [128, 16, 128], BF16)
            self.b_bc = sb("b_bc", [128, 16, 128], F32)
            self.invtab = sb("invtab", [128, 4, 16], F32)
            self.qh = sb("qh", [128, NCH, 2], F32)
            self.cstash = sb("cstash", [128, 2, 512], BF16)
            self.cm = sb("cm", [128, 12, 128], BF16)
            self.small = sb("small", [128, 64], F32)
            self.ps = es.enter_context(nc.psum_tensor("ps", [128, 8 * 512], F32))

            sem_names = [PE, ACT, DVE, POOL, "cst", "ld0", "ld1", "st0", "st1"] + ["w%d" % i for i in range(N_WSEM)]
            sems = {n: es.enter_context(nc.semaphore(n)) for n in sem_names}
            block = es.enter_context(nc.Block())

            self.record()
            P = self.P
            P.finalize()
            fin = [(s, P.dma_cum[s]) for s in ("st0", "st1") if s in P.dma_cum]

            @block.sync
            def _(e):
                P.emit(SP, e, sems, final_waits=fin)

            @block.tensor
            def _(e):
                P.emit(PE, e, sems)

            @block.scalar
            def _(e):
                P.emit(ACT, e, sems)

            @block.vector
            def _(e):
                P.emit(DVE, e, sems)

            @block.gpsimd
            def _(e):
                P.emit(POOL, e, sems)
        return nc

    def record(self):
        self.setup()
        stages = []
        for tile in range(self.n_tiles):
            stages.append(dict(w=None, c=lambda pre, nxt, tile=tile: self.load_x(tile), norm=None, out=False))
            for l in self.layers:
                kind = l % 3
                j = l // 3
                mixn = (R_MIX + l * 8, "h")
                if kind == 0:
                    stages.append(dict(w=lambda j=j: self.wblk_A(j),
                                       c=lambda wb, pre, nxt, l=l, j=j: self.mixer_A(l, j, wb, pre, nxt), norm=mixn, out=True))
                elif kind == 1:
                    for part in range(2):
                        stages.append(dict(w=lambda part=part: self.wblk_B(part),
                                           c=lambda wb, pre, nxt, l=l, part=part, tile=tile: self.mixer_B(l, part, wb, tile, pre, nxt),
                                           norm=(mixn if part == 0 else None), out=True))
                else:
                    for part in range(2):
                        stages.append(dict(w=lambda part=part: self.wblk_C(part),
                                           c=lambda wb, pre, nxt, l=l, part=part, tile=tile: self.mixer_C(l, part, wb, tile, pre, nxt),
                                           norm=(mixn if part == 0 else None), out=True))
                j0 = 0
                for pi, nj in enumerate(FFN_PARTS):
                    stages.append(dict(w=lambda l=l, j0=j0, nj=nj: self.wblk_F(l, j0, nj),
                                       c=lambda wb, pre, nxt, l=l, pi=pi, nj=nj: self.ffn_part(l, pi, nj, wb, pre, nxt),
                                       norm=((R_FFN + l * 8, "h") if pi == 0 else None), out=True))
                    j0 += nj
            stages.append(dict(w=None, c=lambda pre, nxt, tile=tile: self.store_x(tile, pre),
                               norm=((R_FIN, "y") if self.final_norm else None), out=False))
        pending = {}
        widx = [i for i, st in enumerate(stages) if st["w"] is not None]
        nxt_of = dict(zip(widx[:-1], widx[1:]))
        if widx:
            pending[widx[0]] = stages[widx[0]]["w"]()
        done_norm = False
        for i, st in enumerate(stages):
            pre = None if done_norm else st["norm"]
            nxt = None
            if st["out"] and i + 1 < len(stages):
                nxt = stages[i + 1]["norm"]
            done_norm = nxt is not None
            if st["w"] is None:
                st["c"](pre, nxt)
                continue
            n = nxt_of.get(i)
            if n is not None:
                pending[n] = stages[n]["w"]()
            wb = pending.pop(i)
            st["c"](wb, pre, nxt)
            wb["blk"].open = False

    def setup(self):
        P, dr = self.P, self.dr
        grp = [0]
        vst = self.ev[0:NVEC, 1, 0:128]
        P.add(SP, lambda e: e.dma_start(out=vst, in_=dr["vecs"][:, :]), writes=[("ev", 1)], dma_sem="cst", group=grp)
        has_a = "a_w_s" in dr
        wst = self.xT[:, 0:2, :].rearrange("p s (g j) -> p (s g) j", j=128)
        iok = [("x", c, b) for c in range(2) for b in range(NBLK)]
        if has_a:
            P.add(SP, lambda e: e.dma_start(out=wst, in_=dr["a_w_s"].rearrange("l g i j -> i (l g) j")),
                  writes=iok, dma_sem="cst", group=grp)
            P.add(SP, lambda e: e.dma_start(out=self.b_bc[:].rearrange("p g i -> p (g i)"),
                                            in_=dr["a_b_s"].partition_broadcast(128)),
                  writes=["b_bc"], dma_sem="cst", group=grp)
        onesf = self.ev[:, 0, 0:128]
        P.add(POOL, lambda e: e.memset(onesf, 1.0), writes=[("ev", 0)])
        P.add(POOL, lambda e: e.affine_select(out=self.ident[:], in_=onesf, pattern=[[-1, 128]],
                                              compare_op=ALU.is_equal, fill=0.0, base=0, channel_multiplier=1),
              reads=[("ev", 0)], writes=["ident"])
        P.add(DVE, lambda e: e.memset(self.ones_bf[:], 1.0), writes=["ones"])
        P.add(DVE, lambda e: e.memset(self.small[:, 48:49], -0.5), writes=["neghalf"])
        for g, w in enumerate(POOL_WINDOWS):
            P.add(DVE, lambda e, g=g, w=w: e.memset(self.invtab[:, g, :], 1.0 / w), writes=["invtab"])
            for t in range(w - 1):
                P.add(DVE, lambda e, g=g, t=t: e.memset(self.invtab[:, g, t:t + 1], 1.0 / (t + 1)), writes=["invtab"])
        if "c_w_in" in dr:
            band = self.ev[:, 2, 0:128]
            band2 = self.ev[:, 2, 128:256]
            tmpm = self.ev[:, 2, 256:384]
            invf = self.ev[:, 2, 384:512]
            ek = [("ev", 2)]
            for g, w in enumerate(POOL_WINDOWS):
                P.add(POOL, lambda e: e.affine_select(out=band, in_=onesf, pattern=[[1, 128]], compare_op=ALU.is_ge, fill=0.0,
                                                      base=0, channel_multiplier=-1), reads=[("ev", 0)], writes=ek)
                P.add(POOL, lambda e, w=w: e.affine_select(out=band2, in_=band, pattern=[[-1, 128]], compare_op=ALU.is_ge, fill=0.0,
                                                           base=w - 1, channel_multiplier=1), reads=ek, writes=ek)
                P.add(DVE, lambda e, g=g, w=w: e.scalar_tensor_tensor(out=self.cm[:, g * 3 + 0, :], in0=band2, scalar=1.0 / w, in1=self.ident[:],
                                                                      op0=ALU.mult, op1=ALU.subtract), reads=ek + ["ident"], writes=["cm"])
                P.add(POOL, lambda e, w=w: e.affine_select(out=tmpm, in_=onesf, pattern=[[-1, 128]], compare_op=ALU.is_ge, fill=0.0,
                                                           base=-(129 - w), channel_multiplier=1), reads=[("ev", 0)] + ek, writes=ek)
                P.add(DVE, lambda e, g=g, w=w: e.tensor_scalar(out=self.cm[:, g * 3 + 1, :], in0=tmpm, scalar1=1.0 / w, scalar2=None, op0=ALU.mult),
                      reads=ek, writes=["cm"])
                P.add(DVE, lambda e, w=w: e.memset(invf, 1.0 / w), reads=ek, writes=ek)
                P.add(DVE, lambda e, g=g: e.tensor_copy(invf[:, 0:16], self.invtab[:, g, :]), reads=ek + ["invtab"], writes=ek)
                P.add(DVE, lambda e: e.tensor_tensor(out=tmpm, in0=band2, in1=invf, op=ALU.mult), reads=ek, writes=ek)
                P.add(DVE, lambda e, g=g: e.tensor_tensor(out=self.cm[:, g * 3 + 2, :], in0=tmpm, in1=self.ident[:], op=ALU.subtract),
                      reads=ek + ["ident"], writes=["cm"])
        b = self.bank()
        P.add(PE, lambda e: e.transpose(self.bap(b)[:, 0:NVEC], vst, self.ident[0:NVEC, 0:NVEC]),
              reads=[("ev", 1), "ident"], writes=[("ps", b)])
        P.add(DVE, lambda e: e.tensor_copy(self.vecs[:], self.bap(b)[:, 0:NVEC]), reads=[("ps", b)], writes=["vecs"])
        if has_a:
            P.add(DVE, lambda e: e.memset(wst[0:64, :, 64:128], 0.0), reads=iok, writes=iok)
            for q in range(4):
                b = self.bank()
                for r in range(4):
                    lg = q * 4 + r
                    P.add(PE, lambda e, b=b, r=r, lg=lg: e.transpose(self.bap(b)[:, r * 128:(r + 1) * 128], wst[:, lg, :], self.ident[:]),
                          reads=iok + ["ident"], writes=[("ps", b)])
                P.add(ACT, lambda e, b=b, q=q: e.activation(out=self.wmT[:, q * 4:(q + 1) * 4, :],
                                                            in_=self.bap(b).rearrange("p (r i) -> p r i", r=4), func=AF.Copy),
                      reads=[("ps", b)], writes=["wmT"])
        self.scr_keys = []

    def ykeys(self, c):
        if c < 4:
            return [("a", 2 * c + i, tt) for i in range(2) for tt in range(NSUB)]
        q = c - 4
        ks = [("vn", 2 * q), ("vn", 2 * q + 1)]
        if q == 0:
            ks += [("pt", b) for b in range(4)]
        elif q == 1:
            ks += [("pt", b) for b in range(4, 8)]
        elif q == 2:
            ks += [("d", i) for i in range(4)]
        return ks

    def yview(self, c):
        return self.yA[:, c, :] if c < 4 else self.yB[:, c - 4, :]

    def load_x(self, tile):
        P, dr = self.P, self.dr
        s, half = divmod(tile, SEQ // TILE)
        self.run_hook()
        if half == 0:
            P.add(DVE, lambda e: e.memset(self.qh[:], 0.0), writes=["qh"])
        t0 = half * TILE
        for tt in range(NSUB):
            src = dr["x"][s, :, t0 + tt * SUB:t0 + (tt + 1) * SUB].rearrange("(c p) t -> p c t", p=128)
            dst = self.xT[:, :, tt * SUB:(tt + 1) * SUB]
            wk = [k for c in range(NCH) for k in self.xk(c, tt)]
            P.add(SP, lambda e, dst=dst, src=src: e.dma_start(out=dst, in_=src), writes=wk, dma_sem="ld%d" % tt)
        if self.pending_store is not None:
            f, self.pending_store = self.pending_store, None
            f()

    def store_x(self, tile, pre=None):
        P, dr = self.P, self.dr
        s, half = divmod(tile, SEQ // TILE)
        t0 = half * TILE
        self.run_hook()
        if self.final_norm:
            if pre is not None:
                for tt in range(NSUB):
                    self.rmsnorm(tt, pre[0], "y")

            def emit():
                for hh in range(2):
                    dst = dr["out"][s, hh * 512:(hh + 1) * 512, t0:t0 + TILE].rearrange("(c p) t -> p c t", p=128)
                    src = self.yA[:] if hh == 0 else self.yB[:]
                    rk = [k for c in range(hh * 4, hh * 4 + 4) for k in self.ykeys(c)]
                    P.add(SP, lambda e, dst=dst, src=src: e.dma_start(out=dst, in_=src), reads=rk,
                          writes=[("out", tile, hh)], dma_sem="st%d" % hh)
        else:
            def emit():
                for hh in range(2):
                    dst = dr["out"][s, hh * 512:(hh + 1) * 512, t0:t0 + TILE].rearrange("(c p) t -> p c t", p=128)
                    src = self.xT[:, hh * 4:(hh + 1) * 4, :]
                    rk = [("x", c, b) for c in range(hh * 4, hh * 4 + 4) for b in range(NBLK)]
                    P.add(SP, lambda e, dst=dst, src=src: e.dma_start(out=dst, in_=src), reads=rk,
                          writes=[("out", tile, hh)], dma_sem="st%d" % hh)
        if self.final_norm and tile + 1 < self.n_tiles:
            self.pending_store = emit
        else:
            emit()

    def rmsnorm(self, tt, gcol, mode="h"):
        n = self.norm_stats(tt)
        self.norm_apply(tt, n, gcol, mode)

    def norm_stats(self, tt):
        self.norm_sq(tt)
        return self.norm_mm(tt)

    def norm_sq(self, tt):
        P = self.P
        ts = slice(tt * SUB, (tt + 1) * SUB)
        for c in range(NCH):
            xin = self.xT[:, c, ts]
            xo = self.xsq[:, c, :]
            P.add(ACT, lambda e, xo=xo, xin=xin: e.activation(out=xo, in_=xin, func=AF.Square),
                  reads=self.xk(c, tt), writes=[("xsq", c)])

    def norm_mm(self, tt):
        P = self.P
        bs = self.bank()
        for c in range(NCH):
            xo = self.xsq[:, c, :]
            P.add(PE, lambda e, xo=xo, c=c: e.matmul(self.bap(bs), self.ones_bf[:], xo, start=(c == 0), stop=(c == NCH - 1)),
                  reads=[("xsq", c), "ones"], writes=[("ps", bs)])
        n = self.nrm_rr
        self.nrm_rr ^= 1
        nv = self.nrm[:, n, :]
        P.add(ACT, lambda e: e.activation(out=nv, in_=self.bap(bs), func=AF.Ln, scale=1.0 / D, bias=EPS),
              reads=[("ps", bs)], writes=[("nrm", n)])
        P.add(ACT, lambda e: e.activation(out=nv, in_=nv, func=AF.Exp, scale=-0.5),
              reads=[("nrm", n)], writes=[("nrm", n)])
        return n

    def norm_apply(self, tt, n, gcol, mode="h"):
        P = self.P
        ts = slice(tt * SUB, (tt + 1) * SUB)
        nv = self.nrm[:, n, :]
        order = (4, 5, 6, 7, 2, 3, 0, 1) if mode == "y" else range(NCH)
        for c in order:
            xin = self.xT[:, c, ts]
            gc = self.vecs[:, gcol + c:gcol + c + 1]
            if mode == "y":
                dst, wk = self.yview(c)[:, ts], self.ykeys(c)
            else:
                dst, wk = self.h[:, c, ts], [("h", c, tt)]
            P.add(DVE, lambda e, dst=dst, xin=xin, gc=gc: e.scalar_tensor_tensor(out=dst, in0=xin, scalar=gc, in1=nv,
                                                                                 op0=ALU.mult, op1=ALU.mult),
                  reads=self.xk(c, tt) + [("nrm", n), "vecs"], writes=wk)

    def outproj(self, wout, nj, wkey, next_norm=None):
        P = self.P

        def grp_mm(tt, m):
            ts = slice(tt * SUB, (tt + 1) * SUB)
            b = self.bank()
            for j in range(nj):
                P.add(PE, lambda e, b=b, j=j, m=m, ts=ts: e.matmul(self.bap(b), wout[:, j, m * 128:(m + 1) * 128],
                                                                  self.abuf[:, j, ts], start=(j == 0), stop=(j == nj - 1)),
                      reads=[wkey, ("a", j, tt)], writes=[("ps", b)])
            return b

        def grp_upd(tt, m, b):
            xv = self.xT[:, m, tt * SUB:(tt + 1) * SUB]
            P.add(DVE, lambda e, b=b, xv=xv: e.tensor_tensor(out=xv, in0=self.bap(b), in1=xv, op=ALU.add),
                  reads=[("ps", b)] + self.xk(m, tt), writes=self.xk(m, tt))

        def grp(tt, m):
            grp_upd(tt, m, grp_mm(tt, m))

        for m in range(NCH):
            grp(0, m)
        if next_norm is None:
            for m in range(NCH):
                grp(1, m)
            return
        gcol, mode = next_norm
        for m in range(2):
            grp(1, m)
        n0 = self.norm_stats(0)
        banks = [grp_mm(1, m) for m in range(2, NCH)]
        self.norm_apply(0, n0, gcol, mode)
        for i, m in enumerate(range(2, NCH)):
            grp_upd(1, m, banks[i])
        if mode == "y":
            self.rmsnorm(1, gcol, mode)
        else:
            self.norm_sq(1)
            self.hook = lambda: self.norm_apply(1, self.norm_mm(1), gcol, mode)

    def wblk_F(self, l, j0, nj):
        dr = self.dr
        ncol = nj * 128
        size = 2 * NCH * ncol + nj * D
        blk, over = self.walloc(size)
        wg = self.wview(blk, 0, NCH * ncol).rearrange("p (k n) -> p k n", k=NCH)
        wu = self.wview(blk, NCH * ncol, NCH * ncol).rearrange("p (k n) -> p k n", k=NCH)
        wd = self.wview(blk, 2 * NCH * ncol, nj * D).rearrange("p (j n) -> p j n", j=nj)
        c0, c1 = j0 * 128, (j0 + nj) * 128
        ks = self.wdma(blk, over, [
            (wg, dr["f_w_gate"][l, :, c0:c1].rearrange("(k p) n -> p k n", p=128), "g"),
            (wu, dr["f_w_up"][l, :, c0:c1].rearrange("(k p) n -> p k n", p=128), "u"),
            (wd, dr["f_w_down"][l, c0:c1, :].rearrange("(j p) n -> p j n", p=128), "d"),
        ])
        return dict(blk=blk, wg=wg, wu=wu, wd=wd, ks=ks)

    def ffn_part(self, l, pi, nj, wb, pre=None, nxt=None):
        P = self.P
        ks = wb["ks"]
        wg, wu, wd = wb["wg"], wb["wu"], wb["wd"]
        self.pre_norm(pre)
        for tt in range(NSUB):
            ts = slice(tt * SUB, (tt + 1) * SUB)
            for j in range(nj):
                bg = self.bank()
                bu = self.bank()
                for k in range(NCH):
                    P.add(PE, lambda e, bg=bg, k=k, j=j, ts=ts: e.matmul(self.bap(bg), wg[:, k, j * 128:(j + 1) * 128], self.h[:, k, ts],
                                                                        start=(k == 0), stop=(k == NCH - 1)),
                          reads=[ks["g"], ("h", k, tt)], writes=[("ps", bg)])
                for k in range(NCH):
                    P.add(PE, lambda e, bu=bu, k=k, j=j, ts=ts: e.matmul(self.bap(bu), wu[:, k, j * 128:(j + 1) * 128], self.h[:, k, ts],
                                                                        start=(k == 0), stop=(k == NCH - 1)),
                          reads=[ks["u"], ("h", k, tt)], writes=[("ps", bu)])
                s = self.evslot()
                sg = self.ev[:, s, :]
                P.add(ACT, lambda e, sg=sg, bg=bg: e.activation(out=sg, in_=self.bap(bg), func=AF.Silu),
                      reads=[("ps", bg)], writes=[("ev", s)])
                av = self.abuf[:, j, ts]
                P.add(DVE, lambda e, av=av, bu=bu, sg=sg: e.tensor_tensor(out=av, in0=self.bap(bu), in1=sg, op=ALU.mult),
                      reads=[("ps", bu), ("ev", s)], writes=[("a", j, tt)])
                if tt == 0 and j == 1:
                    self.run_hook()
        self.outproj(wd, nj, ks["d"], nxt)

    def wblk_A(self, j):
        dr = self.dr
        size = NCH * 2 * D + NCH * D
        blk, over = self.walloc(size)
        win = self.wview(blk, 0, NCH * 2 * D).rearrange("p (k n) -> p k n", k=NCH)
        wout = self.wview(blk, NCH * 2 * D, NCH * D).rearrange("p (k n) -> p k n", k=NCH)
        ks = self.wdma(blk, over, [
            (win[:, :, D:2 * D], dr["a_w_in"][j, :, D:2 * D].rearrange("(k p) n -> p k n", p=128), "v"),
            (win[:, :, 0:D], dr["a_w_in"][j, :, 0:D].rearrange("(k p) n -> p k n", p=128), "u"),
            (wout, dr["a_w_out"][j].rearrange("(k p) n -> p k n", p=128), "o"),
        ])
        return dict(blk=blk, win=win, wout=wout, ks=ks)

    def mixer_A(self, l, j, wb, pre=None, nxt=None):
        P = self.P
        ks = wb["ks"]
        win, wout = wb["win"], wb["wout"]
        sm = self.small
        self.pre_norm(pre)
        vn = self.scrB[:].rearrange("p (b n) -> p b n", b=NBLK)
        vraw = self.scrF[:, 0:2048].rearrange("p (s n) -> p s n", s=2)
        scr = self.scr_keys
        def v_norm(blk):
            vs = blk % 2
            mean = sm[:, 24 + 2 * blk:25 + 2 * blk]
            rs = sm[:, 40 + blk:41 + blk]
            P.add(DVE, lambda e, blk=blk, vs=vs, mean=mean, rs=rs: e.tensor_scalar(out=vn[:, blk, :], in0=vraw[:, vs, :], scalar1=mean, scalar2=rs,
                                                                               op0=ALU.subtract, op1=ALU.mult),
                  reads=[("vraw", vs, 0), ("vraw", vs, 1), ("mv", blk), ("rs", blk)], writes=[("vn", blk)])

        for blk in range(NBLK):
            tt = blk // 4
            tks = slice(blk * 128, (blk + 1) * 128)
            vs = blk % 2
            for hh in range(2):
                b = self.bank()
                for k in range(NCH):
                    P.add(PE, lambda e, b=b, k=k, hh=hh, tks=tks: e.matmul(self.bap(b), self.h[:, k, tks],
                                                                          win[:, k, D + hh * 512:D + (hh + 1) * 512],
                                                                          start=(k == 0), stop=(k == NCH - 1)),
                          reads=[ks["v"], ("h", k, tt)], writes=[("ps", b)])
                vo = vraw[:, vs, hh * 512:(hh + 1) * 512]
                P.add(ACT, lambda e, vo=vo, b=b: e.activation(out=vo, in_=self.bap(b), func=AF.Gelu),
                      reads=[("ps", b)], writes=[("vraw", vs, hh)])
                st = sm[:, vs * 12 + hh * 6:vs * 12 + (hh + 1) * 6]
                P.add(DVE, lambda e, st=st, vo=vo: e.bn_stats(st, vo), reads=[("vraw", vs, hh)], writes=[("st", vs, hh)])
            mv = sm[:, 24 + 2 * blk:26 + 2 * blk]
            P.add(DVE, lambda e, mv=mv, vs=vs: e.bn_aggr(mv, sm[:, vs * 12:vs * 12 + 12]), reads=[("st", vs, 0), ("st", vs, 1)], writes=[("mv", blk)])
            rs = sm[:, 40 + blk:41 + blk]
            P.add(POOL, lambda e, rs=rs, blk=blk: e.tensor_scalar(out=rs, in0=sm[:, 25 + 2 * blk:26 + 2 * blk], scalar1=EPS, scalar2=1.0,
                                                                op0=ALU.add, op1=ALU.mult),
                  reads=[("mv", blk)], writes=[("rs", blk)])
            P.add(POOL, lambda e, rs=rs: e.tensor_tensor(out=rs, in0=rs, in1=sm[:, 48:49], op=ALU.pow), reads=[("rs", blk), "neghalf"], writes=[("rs", blk)])
            if blk >= 1:
                v_norm(blk - 1)
            if blk == 1:
                self.run_hook()
        v_norm(NBLK - 1)
        for tt in range(NSUB):
            ts = slice(tt * SUB, (tt + 1) * SUB)
            for g in range(NCH):
                bu = self.bank()
                for k in range(NCH):
                    P.add(PE, lambda e, bu=bu, k=k, g=g, ts=ts: e.matmul(self.bap(bu), win[:, k, g * 128:(g + 1) * 128], self.h[:, k, ts],
                                                                        start=(k == 0), stop=(k == NCH - 1)),
                          reads=[ks["u"], ("h", k, tt)], writes=[("ps", bu)])
                s1 = self.evslot()
                ub = self.ev[:, s1, :]
                P.add(ACT, lambda e, ub=ub, bu=bu: e.activation(out=ub, in_=self.bap(bu), func=AF.Gelu),
                      reads=[("ps", bu)], writes=[("ev", s1)])
                bsp = self.bank()
                for r in range(4):
                    blk = tt * 4 + r
                    P.add(PE, lambda e, bsp=bsp, r=r, blk=blk, g=g: e.matmul(self.bap(bsp)[:, r * 128:(r + 1) * 128],
                                                                            vn[:, blk, g * 128:(g + 1) * 128],
                                                                            self.wmT[:, j * 8 + g, :], start=True, stop=True),
                          reads=[("vn", blk), "wmT"], writes=[("ps", bsp)])
                s2 = self.evslot()
                tb = self.ev[:, s2, :]
                gam = self.vecs[:, R_AVG + j * 8 + g:R_AVG + j * 8 + g + 1]
                bb = self.b_bc[:, j * 8 + g, :].unsqueeze(1).to_broadcast([128, 4, 128])
                P.add(DVE, lambda e, tb=tb, bsp=bsp, gam=gam, bb=bb: e.scalar_tensor_tensor(
                    out=tb.rearrange("p (r i) -> p r i", r=4), in0=self.bap(bsp).rearrange("p (r i) -> p r i", r=4),
                    scalar=gam, in1=bb, op0=ALU.mult, op1=ALU.add),
                      reads=[("ps", bsp), "vecs", "b_bc"], writes=[("ev", s2)])
                av = self.abuf[:, g, ts]
                P.add(DVE, lambda e, av=av, tb=tb, ub=ub: e.tensor_tensor(out=av, in0=tb, in1=ub, op=ALU.mult),
                      reads=[("ev", s1), ("ev", s2)], writes=[("a", g, tt)])
        self.outproj(wout, NCH, ks["o"], nxt)

    def wblk_B(self, part):
        dr = self.dr
        nc4 = 4 * 128
        size = 3 * NCH * nc4 + 4 * D
        blk, over = self.walloc(size)
        ws = []
        pieces = []
        for i in range(3):
            v = self.wview(blk, i * NCH * nc4, NCH * nc4).rearrange("p (k n) -> p k n", k=NCH)
            ws.append(v)
            c0 = i * D + part * nc4
            pieces.append((v, dr["b_w_in"][0, :, c0:c0 + nc4].rearrange("(k p) n -> p k n", p=128), "i%d" % i))
        wout = self.wview(blk, 3 * NCH * nc4, 4 * D).rearrange("p (j n) -> p j n", j=4)
        pieces.append((wout, dr["b_w_out"][0, part * nc4:(part + 1) * nc4, :].rearrange("(j p) n -> p j n", p=128), "o"))
        ks = self.wdma(blk, over, pieces)
        return dict(blk=blk, wb_=ws[0], wc=ws[1], wx=ws[2], wout=wout, ks=ks)

    def mixer_B(self, l, part, wb, tile, pre=None, nxt=None):
        P = self.P
        ks = wb["ks"]
        scr = self.scr_keys
        self.pre_norm(pre)
        qb = self.scrF[:, 0:2 * 514].rearrange("p (s n) -> p s n", s=2)
        yb = self.scrF[:, 1028:1028 + 1024].rearrange("p (s n) -> p s n", s=2)
        rr = 0
        for tt in range(NSUB):
            ts = slice(tt * SUB, (tt + 1) * SUB)
            for ci in range(4):
                c = part * 4 + ci
                banks = []
                for wi, w in enumerate((wb["wb_"], wb["wc"], wb["wx"])):
                    wkey = ks["i%d" % wi]
                    b = self.bank()
                    banks.append(b)
                    for k in range(NCH):
                        P.add(PE, lambda e, b=b, k=k, w=w, ci=ci, ts=ts: e.matmul(self.bap(b), w[:, k, ci * 128:(ci + 1) * 128], self.h[:, k, ts],
                                                                                 start=(k == 0), stop=(k == NCH - 1)),
                              reads=[wkey, ("h", k, tt)], writes=[("ps", b)])
                bgb, bgc, bxt = banks
                s = self.evslot()
                xs = self.ev[:, s, :]
                P.add(ACT, lambda e, xs=xs, bxt=bxt: e.activation(out=xs, in_=self.bap(bxt), func=AF.Copy),
                      reads=[("ps", bxt)], writes=[("ev", s)])
                qs = rr
                rr ^= 1
                q = qb[:, qs, :]
                yv = yb[:, qs, :]
                hal = self.qh[:, c, :]
                P.add(DVE, lambda e, q=q, hal=hal: e.tensor_copy(q[:, 0:2], hal), reads=["qh"], writes=[("q", qs)] + scr)
                P.add(DVE, lambda e, q=q, bgc=bgc, xs=xs: e.tensor_tensor(out=q[:, 2:514], in0=self.bap(bgc), in1=xs, op=ALU.mult),
                      reads=[("ps", bgc), ("ev", s)], writes=[("q", qs)])
                P.add(DVE, lambda e, q=q, hal=hal: e.tensor_copy(hal, q[:, 512:514]), reads=[("q", qs)], writes=["qh"])
                w0 = self.vecs[:, R_CONV + 0 * 8 + c:R_CONV + 0 * 8 + c + 1]
                w1 = self.vecs[:, R_CONV + 1 * 8 + c:R_CONV + 1 * 8 + c + 1]
                w2 = self.vecs[:, R_CONV + 2 * 8 + c:R_CONV + 2 * 8 + c + 1]
                P.add(DVE, lambda e, yv=yv, q=q, w0=w0: e.tensor_scalar(out=yv, in0=q[:, 0:512], scalar1=w0, scalar2=None, op0=ALU.mult),
                      reads=[("q", qs), "vecs"], writes=[("y", qs)] + scr)
                P.add(DVE, lambda e, yv=yv, q=q, w1=w1: e.scalar_tensor_tensor(out=yv, in0=q[:, 1:513], scalar=w1, in1=yv,
                                                                              op0=ALU.mult, op1=ALU.add),
                      reads=[("q", qs), ("y", qs), "vecs"], writes=[("y", qs)])
                P.add(DVE, lambda e, yv=yv, q=q, w2=w2: e.scalar_tensor_tensor(out=yv, in0=q[:, 2:514], scalar=w2, in1=yv,
                                                                              op0=ALU.mult, op1=ALU.add),
                      reads=[("q", qs), ("y", qs), "vecs"], writes=[("y", qs)])
                av = self.abuf[:, ci, ts]
                P.add(DVE, lambda e, av=av, bgb=bgb, yv=yv: e.tensor_tensor(out=av, in0=self.bap(bgb), in1=yv, op=ALU.mult),
                      reads=[("ps", bgb), ("y", qs)], writes=[("a", ci, tt)])
                if tt == 0 and ci == 1:
                    self.run_hook()
        self.outproj(wb["wout"], 4, ks["o"], nxt)

    def wblk_C(self, part):
        dr = self.dr
        nc4 = 4 * 128
        size = NCH * nc4 + 4 * 256 + 4 * D
        blk, over = self.walloc(size)
        win = self.wview(blk, 0, NCH * nc4).rearrange("p (k n) -> p k n", k=NCH)
        wgr = self.wview(blk, NCH * nc4, 4 * 256).rearrange("p (g n) -> p g n", g=4)
        wout = self.wview(blk, NCH * nc4 + 4 * 256, 4 * D).rearrange("p (j n) -> p j n", j=4)
        ks = self.wdma(blk, over, [
            (win, dr["c_w_in"][0, :, part * nc4:(part + 1) * nc4].rearrange("(k p) n -> p k n", p=128), "i"),
            (wgr, dr["c_w_grp"][0, part * 2:part * 2 + 2].rearrange("g (b p) d -> p (g b) d", p=128), "g"),
            (wout, dr["c_w_out"][0, part * nc4:(part + 1) * nc4, :].rearrange("(j p) n -> p j n", p=128), "o"),
        ])
        return dict(blk=blk, win=win, wgr=wgr, wout=wout, ks=ks)

    def mixer_C(self, l, part, wb, tile, pre=None, nxt=None):
        P = self.P
        ks = wb["ks"]
        first = (tile % (SEQ // TILE) == 0)
        self.pre_norm(pre)
        ptok = self.scrB[:, 0:NBLK * 512].rearrange("p (b n) -> p b n", b=NBLK)
        db = self.scrB[:, NBLK * 512:NBLK * 512 + 4 * 512].rearrange("p (s n) -> p s n", s=4)
        win, wgr = wb["win"], wb["wgr"]
        for tt in range(NSUB):
            ts = slice(tt * SUB, (tt + 1) * SUB)
            for r in range(4):
                blk = tt * 4 + r
                tks = slice(blk * 128, (blk + 1) * 128)
                b = self.bank()
                for k in range(NCH):
                    P.add(PE, lambda e, b=b, k=k, tks=tks: e.matmul(self.bap(b), self.h[:, k, tks], win[:, k, :],
                                                                   start=(k == 0), stop=(k == NCH - 1)),
                          reads=[ks["i"], ("h", k, tt)], writes=[("ps", b)])
                pv = ptok[:, blk, :]
                if r % 2 == 0:
                    P.add(ACT, lambda e, pv=pv, b=b: e.activation(out=pv, in_=self.bap(b), func=AF.Copy),
                          reads=[("ps", b)], writes=[("pt", blk)])
                else:
                    P.add(DVE, lambda e, pv=pv, b=b: e.tensor_copy(pv, self.bap(b)), reads=[("ps", b)], writes=[("pt", blk)])
                if blk == 1:
                    self.run_hook()
            for ci in range(4):
                c = part * 4 + ci
                gi = ci // 2
                g = part * 2 + gi
                cs = slice(ci * 128, (ci + 1) * 128)
                b2 = self.bank()
                for r in range(4):
                    blk = tt * 4 + r
                    fb = first and blk == 0
                    m0 = self.cm[:, g * 3 + (2 if fb else 0), :]
                    o = self.bap(b2)[:, r * 128:(r + 1) * 128]
                    P.add(PE, lambda e, o=o, blk=blk, cs=cs, m0=m0, fb=fb: e.matmul(o, ptok[:, blk, cs], m0, start=True, stop=fb),
                          reads=[("pt", blk), "cm"], writes=[("ps", b2)])
                    if not fb:
                        if blk > 0:
                            pl, pk = ptok[:, blk - 1, cs], ("pt", blk - 1)
                        else:
                            pl, pk = self.cstash[:, part, cs], ("cst", part)
                        m1 = self.cm[:, g * 3 + 1, :]
                        P.add(PE, lambda e, o=o, pl=pl, m1=m1: e.matmul(o, pl, m1, start=False, stop=True),
                              reads=[pk, "cm"], writes=[("ps", b2)])
                dv = db[:, ci, :]
                P.add(DVE, lambda e, dv=dv, b2=b2: e.tensor_copy(dv, self.bap(b2)), reads=[("ps", b2)], writes=[("d", ci)])
                if ci % 2 == 1:
                    for aq in range(2):
                        co = gi * 2 + aq
                        b3 = self.bank()
                        for bq in range(2):
                            P.add(PE, lambda e, b3=b3, bq=bq, aq=aq, gi=gi: e.matmul(self.bap(b3), wgr[:, gi * 2 + bq, aq * 128:(aq + 1) * 128],
                                                                                    db[:, gi * 2 + bq, :], start=(bq == 0), stop=(bq == 1)),
                                  reads=[ks["g"], ("d", gi * 2 + bq)], writes=[("ps", b3)])
                        av = self.abuf[:, co, ts]
                        sc = self.vecs[:, R_CSC + part * 4 + co:R_CSC + part * 4 + co + 1]
                        P.add(ACT, lambda e, av=av, b3=b3, sc=sc: e.activation(out=av, in_=self.bap(b3), func=AF.Copy, scale=sc),
                              reads=[("ps", b3), "vecs"], writes=[("a", co, tt)])
        P.add(ACT, lambda e: e.activation(out=self.cstash[:, part, :], in_=ptok[:, NBLK - 1, :], func=AF.Copy),
              reads=[("pt", NBLK - 1)], writes=[("cst", part)])
        self.outproj(wb["wout"], 4, ks["o"], nxt)


_CACHE = {}


def _get_prog(layers, final_norm):
    key = (tuple(layers), final_norm)
    if key not in _CACHE:
        mk = MK(layers, final_norm)
        nc = mk.build()
        _CACHE[key] = (nc, mk.input_names)
    return _CACHE[key]


def _pack_vecs(inp):
    rows = [inp["norm_mix_g"].reshape(32, 128), inp["norm_ffn_g"].reshape(32, 128),
            inp["final_norm_g"].reshape(8, 128), inp["a_v_norm_g"].reshape(16, 128),
            inp["b_conv_w"].reshape(24, 128), inp["c_scale"].reshape(8, 128)]
    return np.ascontiguousarray(np.concatenate(rows, axis=0).astype(np.float32))


def _run(x, inp, layers, final_norm, n_cores=N_CORES):
    nc, names = _get_prog(layers, final_norm)
    shared = dict(inp)
    shared["vecs"] = _pack_vecs(inp)
    shared["a_b_s"] = np.ascontiguousarray(inp["a_b_s"].reshape(1, -1))
    in_maps = []
    for c in range(n_cores):
        m = {k: np.ascontiguousarray(shared[k]) for k in names if k != "x"}
        m["x"] = np.ascontiguousarray(x[c * SPC:(c + 1) * SPC].transpose(0, 2, 1))
        in_maps.append(m)
    res = run_bass_kernel_spmd(nc, in_maps, core_ids=list(range(n_cores)))
    return np.concatenate([np.ascontiguousarray(r["out"].transpose(0, 2, 1)) for r in res.results], axis=0)


def kernel(**inputs):
    inp = {k: np.asarray(v) for k, v in inputs.items()}
    x = np.ascontiguousarray(inp["x"], dtype=np.float32)
    out = _run(x, inp, list(range(DEPTH)), True)
    return out.astype(np.float32)
```

```python
import numpy as np
from contextlib import ExitStack
import concourse.bass as bass
import concourse.mybir as mybir
from concourse.bass_utils import run_bass_kernel_spmd

F32 = mybir.dt.float32
BF16 = mybir.dt.bfloat16
AF = mybir.ActivationFunctionType
ALU = mybir.AluOpType

N_CORES = 8
D = 1024
NCH = 8
SEQ = 2048
SPC = 2
TILE = 1024
SUB = 512
NSUB = TILE // SUB
NBLK = TILE // 128
FH = 2816
FCH = FH // 128
EPS = 1e-6
POOL_WINDOWS = (2, 4, 8, 16)
DEPTH = 4
FFN_PARTS = (5, 6, 6, 5)

R_MIX, R_FFN, R_FIN, R_AVG, R_CONV, R_CSC, NVEC = 0, 32, 64, 72, 88, 112, 120

WRING_BYTES = 78 * 1024
WELEMS = WRING_BYTES // 2
N_WSEM = 12

PE, ACT, DVE, POOL, SP = "pe", "act", "dve", "pool", "sp"
SAME_ENGINE_SYNC = {"act": True, "dve": True, "pool": True}


class Op:
    __slots__ = ("eng", "fn", "deps", "signal", "count", "dma", "idx")

    def __init__(self, eng, fn, dma=None):
        self.eng = eng
        self.fn = fn
        self.deps = ()
        self.signal = False
        self.count = None
        self.dma = dma
        self.idx = -1


class Res:
    __slots__ = ("w", "r")

    def __init__(self):
        self.w = None
        self.r = {}


class Prog:
    def __init__(self):
        self.ops = []
        self.res = {}
        self.dma_cum = {}

    def add(self, eng, fn, reads=(), writes=(), dma_sem=None, group=None):
        dma = None
        if dma_sem is not None:
            cum = self.dma_cum.get(dma_sem, 0) + 16
            self.dma_cum[dma_sem] = cum
            if group is None:
                group = [cum]
            else:
                group[0] = cum
            dma = [dma_sem, group]
        op = Op(eng, fn, dma)
        op.idx = len(self.ops)
        deps = {}
        res = self.res
        for k in reads:
            r = res.get(k)
            if r is None:
                r = res[k] = Res()
            if r.w is not None:
                deps[r.w.idx] = r.w
        for k in writes:
            r = res.get(k)
            if r is None:
                r = res[k] = Res()
            if r.w is not None:
                deps[r.w.idx] = r.w
            for o in r.r.values():
                deps[o.idx] = o
        key = dma[0] if dma is not None else eng
        for k in reads:
            res[k].r[key] = op
        for k in writes:
            r = res[k]
            r.w = op
            r.r = {}
        deps.pop(op.idx, None)
        op.deps = tuple(deps.values())
        self.ops.append(op)
        return op

    @staticmethod
    def _implicit(d, op):
        return (d.dma is None and op.dma is None and d.eng == op.eng
                and (d.eng == PE or not SAME_ENGINE_SYNC.get(d.eng, True)))

    def finalize(self):
        for op in self.ops:
            for d in op.deps:
                if d.dma is not None or self._implicit(d, op):
                    continue
                d.signal = True
        cnt = {}
        for op in self.ops:
            if op.dma is None and op.signal:
                cnt[op.eng] = cnt.get(op.eng, 0) + 1
                op.count = cnt[op.eng]
        self.streams = {}
        for op in self.ops:
            self.streams.setdefault(op.eng, []).append(op)

    def emit(self, eng_name, eng, sems, final_waits=()):
        waited = {}
        for op in self.streams.get(eng_name, []):
            need = {}
            for d in op.deps:
                if d.dma is not None:
                    k, v = d.dma[0], d.dma[1][0]
                elif self._implicit(d, op):
                    continue
                else:
                    k, v = d.eng, d.count
                if v > need.get(k, 0):
                    need[k] = v
            for k, v in need.items():
                if waited.get(k, 0) >= v:
                    continue
                eng.wait_ge(sems[k], v)
                waited[k] = v
            ins = op.fn(eng)
            if op.dma is not None:
                ins.then_inc(sems[op.dma[0]], 16)
            elif op.signal:
                ins.then_inc(sems[op.eng], 1)
        for k, v in final_waits:
            if waited.get(k, 0) < v:
                eng.wait_ge(sems[k], v)
                waited[k] = v


class WBlock:
    def __init__(self, bid, off, size):
        self.id = bid
        self.off = off
        self.size = size
        self.open = True
        self.keys = []


class MK:
    def __init__(self, layers, final_norm, n_tiles=SPC * (SEQ // TILE)):
        self.layers = list(layers)
        self.final_norm = final_norm
        self.n_tiles = n_tiles
        self.nc = bass.Bass("TRN2", target_bir_lowering=False)
        self.P = Prog()
        self.bank_rr = 0
        self.ev_rr = 0
        self.xsq_rr = 0
        self.nrm_rr = 0
        self.io_rr = 0
        self.blocks = []
        self.live = []
        self.whead = 0
        self.store_sems = set()
        self.hook = None
        self.pending_store = None
        self.wsem_rr = 0
        self.after_load = []

    def run_hook(self):
        if self.hook is not None:
            f, self.hook = self.hook, None
            f()

    def pre_norm(self, pre):
        if pre is None:
            return
        self.rmsnorm(0, pre[0], pre[1])
        self.norm_sq(1)
        self.hook = lambda: self.norm_apply(1, self.norm_mm(1), pre[0], pre[1])

    def xk(self, c, tt):
        return [("x", c, tt * 4 + b) for b in range(4)]

    def bank(self):
        b = self.bank_rr
        self.bank_rr = (b + 1) % 8
        return b

    def bap(self, b):
        return self.ps[:, b * 512:(b + 1) * 512]

    def evslot(self):
        s = self.ev_rr
        self.ev_rr = (s + 1) % 3
        return s

    def walloc(self, size):
        assert size <= WELEMS
        off = 0 if (len(self.blocks) % 2 == 0) else WELEMS - size
        over = []
        keep = []
        for b in self.live:
            if b.off < off + size and off < b.off + b.size:
                assert not b.open, "weight ring too small: overlaps an open block"
                over.append(b)
            else:
                keep.append(b)
        blk = WBlock(len(self.blocks), off, size)
        self.blocks.append(blk)
        keep.append(blk)
        self.live = keep
        self.whead = off + size
        return blk, over

    def wdma(self, blk, over, pieces):
        for i, (dst, src, name) in enumerate(pieces):
            sem = "w%d" % (self.wsem_rr % N_WSEM)
            self.wsem_rr += 1
            key = ("wb", blk.id, name)
            blk.keys.append(key)
            wr = [key]
            rd = []
            if i == 0:
                wr += [k for o in over for k in o.keys]
                rd, self.after_load = self.after_load, []
            self.P.add(POOL, lambda e, dst=dst, src=src: e.dma_start(out=dst, in_=src), reads=rd, writes=wr, dma_sem=sem)
        return {name: ("wb", blk.id, name) for (_, _, name) in pieces}

    def wview(self, blk, off, n):
        return self.wring[:, blk.off + off: blk.off + off + n]

    def build(self):
        nc = self.nc
        L = self.layers
        need = lambda kind: any(l % 3 == kind for l in L)
        dr = {}
        dr["x"] = nc.dram_tensor("x", [SPC, D, SEQ], F32, kind="ExternalInput").ap()
        dr["vecs"] = nc.dram_tensor("vecs", [NVEC, 128], F32, kind="ExternalInput").ap()
        if need(0):
            dr["a_w_in"] = nc.dram_tensor("a_w_in", [2, D, 2 * D], F32, kind="ExternalInput").ap()
            dr["a_w_s"] = nc.dram_tensor("a_w_s", [2, 8, 128, 128], F32, kind="ExternalInput").ap()
            dr["a_b_s"] = nc.dram_tensor("a_b_s", [1, 2 * 8 * 128], F32, kind="ExternalInput").ap()
            dr["a_w_out"] = nc.dram_tensor("a_w_out", [2, D, D], F32, kind="ExternalInput").ap()
        if need(1):
            dr["b_w_in"] = nc.dram_tensor("b_w_in", [1, D, 3 * D], F32, kind="ExternalInput").ap()
            dr["b_w_out"] = nc.dram_tensor("b_w_out", [1, D, D], F32, kind="ExternalInput").ap()
        if need(2):
            dr["c_w_in"] = nc.dram_tensor("c_w_in", [1, D, D], F32, kind="ExternalInput").ap()
            dr["c_w_grp"] = nc.dram_tensor("c_w_grp", [1, 4, 256, 256], F32, kind="ExternalInput").ap()
            dr["c_w_out"] = nc.dram_tensor("c_w_out", [1, D, D], F32, kind="ExternalInput").ap()
        dr["f_w_gate"] = nc.dram_tensor("f_w_gate", [DEPTH, D, FH], F32, kind="ExternalInput").ap()
        dr["f_w_up"] = nc.dram_tensor("f_w_up", [DEPTH, D, FH], F32, kind="ExternalInput").ap()
        dr["f_w_down"] = nc.dram_tensor("f_w_down", [DEPTH, FH, D], F32, kind="ExternalInput").ap()
        dr["out"] = nc.dram_tensor("out", [SPC, D, SEQ], F32, kind="ExternalOutput").ap()
        self.dr = dr
        self.input_names = [k for k in dr if k != "out"]

        with ExitStack() as es:
            def sb(name, shape, dt):
                return es.enter_context(nc.sbuf_tensor(name, shape, dt))
            self.xT = sb("xT", [128, NCH, TILE], F32)
            self.hab = sb("hab", [128, 2 * NCH * TILE], BF16)
            self.h = self.hab[:, 0:NCH * TILE].rearrange("p (c t) -> p c t", c=NCH)
            self.abuf = self.hab[:, NCH * TILE:2 * NCH * TILE].rearrange("p (c t) -> p c t", c=NCH)
            self.yA = self.hab[:, NCH * TILE:2 * NCH * TILE].bitcast(F32).rearrange("p (c t) -> p c t", c=4)
            self.wring = sb("wring", [128, WELEMS], BF16)
            self.xsq = sb("xsq", [128, NCH, SUB], BF16)
            self.nrm = sb("nrm", [128, 2, SUB], F32)
            self.ev = sb("ev", [128, 3, SUB], F32)
            self.scrB = sb("scrB", [128, 8192], BF16)
            self.yB = self.scrB[:].bitcast(F32).rearrange("p (c t) -> p c t", c=4)
            self.scrF = sb("scrF", [128, 2112], F32)
            self.vecs = sb("vecsb", [128, NVEC], F32)
            self.ident = sb("ident", [128, 128], F32)
            self.ones_bf = sb("ones_bf", [128, 128], BF16)
            self.wmT = sb("wmT", [128, 16, 128], BF16)
            self.b_bc = sb("b_bc", [128, 16, 128], F32)
            self.invtab = sb("invtab", [128, 4, 16], F32)
            self.qh = sb("qh", [128, NCH, 2], F32)
            self.cstash = sb("cstash", [128, 2, 512], BF16)
            self.cm = sb("cm", [128, 12, 128], BF16)
            self.small = sb("small", [128, 64], F32)
            self.ps = es.enter_context(nc.psum_tensor("ps", [128, 8 * 512], F32))

            sem_names = [PE, ACT, DVE, POOL, "cst", "ld0", "ld1", "st0", "st1"] + ["w%d" % i for i in range(N_WSEM)]
            sems = {n: es.enter_context(nc.semaphore(n)) for n in sem_names}
            block = es.enter_context(nc.Block())

            self.record()
            P = self.P
            P.finalize()
            fin = [(s, P.dma_cum[s]) for s in ("st0", "st1") if s in P.dma_cum]

            @block.sync
            def _(e):
                P.emit(SP, e, sems, final_waits=fin)

            @block.tensor
            def _(e):
                P.emit(PE, e, sems)

            @block.scalar
            def _(e):
                P.emit(ACT, e, sems)

            @block.vector
            def _(e):
                P.emit(DVE, e, sems)

            @block.gpsimd
            def _(e):
                P.emit(POOL, e, sems)
        return nc

    def record(self):
        self.setup()
        stages = []
        for tile in range(self.n_tiles):
            stages.append(dict(w=None, c=lambda pre, nxt, tile=tile: self.load_x(tile), norm=None, out=False))
            for l in self.layers:
                kind = l % 3
                j = l // 3
                mixn = (R_MIX + l * 8, "h")
                if kind == 0:
                    stages.append(dict(w=lambda j=j: self.wblk_A(j),
                                       c=lambda wb, pre, nxt, l=l, j=j: self.mixer_A(l, j, wb, pre, nxt), norm=mixn, out=True))
                elif kind == 1:
                    for part in range(2):
                        stages.append(dict(w=lambda part=part: self.wblk_B(part),
                                           c=lambda wb, pre, nxt, l=l, part=part, tile=tile: self.mixer_B(l, part, wb, tile, pre, nxt),
                                           norm=(mixn if part == 0 else None), out=True))
                else:
                    for part in range(2):
                        stages.append(dict(w=lambda part=part: self.wblk_C(part),
                                           c=lambda wb, pre, nxt, l=l, part=part, tile=tile: self.mixer_C(l, part, wb, tile, pre, nxt),
                                           norm=(mixn if part == 0 else None), out=True))
                j0 = 0
                for pi, nj in enumerate(FFN_PARTS):
                    stages.append(dict(w=lambda l=l, j0=j0, nj=nj: self.wblk_F(l, j0, nj),
                                       c=lambda wb, pre, nxt, l=l, pi=pi, nj=nj: self.ffn_part(l, pi, nj, wb, pre, nxt),
                                       norm=((R_FFN + l * 8, "h") if pi == 0 else None), out=True))
                    j0 += nj
            stages.append(dict(w=None, c=lambda pre, nxt, tile=tile: self.store_x(tile, pre),
                               norm=((R_FIN, "y") if self.final_norm else None), out=False))
        pending = {}
        widx = [i for i, st in enumerate(stages) if st["w"] is not None]
        nxt_of = dict(zip(widx[:-1], widx[1:]))
        if widx:
            pending[widx[0]] = stages[widx[0]]["w"]()
        done_norm = False
        for i, st in enumerate(stages):
            pre = None if done_norm else st["norm"]
            nxt = None
            if st["out"] and i + 1 < len(stages):
                nxt = stages[i + 1]["norm"]
            done_norm = nxt is not None
            if st["w"] is None:
                st["c"](pre, nxt)
                continue
            n = nxt_of.get(i)
            if n is not None:
                pending[n] = stages[n]["w"]()
            wb = pending.pop(i)
            st["c"](wb, pre, nxt)
            wb["blk"].open = False

    def setup(self):
        P, dr = self.P, self.dr
        grp = [0]
        vst = self.ev[0:NVEC, 1, 0:128]
        P.add(SP, lambda e: e.dma_start(out=vst, in_=dr["vecs"][:, :]), writes=[("ev", 1)], dma_sem="cst", group=grp)
        has_a = "a_w_s" in dr
        wst = self.xT[:, 0:2, :].rearrange("p s (g j) -> p (s g) j", j=128)
        iok = [("x", c, b) for c in range(2) for b in range(NBLK)]
        if has_a:
            P.add(SP, lambda e: e.dma_start(out=wst, in_=dr["a_w_s"].rearrange("l g i j -> i (l g) j")),
                  writes=iok, dma_sem="cst", group=grp)
            P.add(SP, lambda e: e.dma_start(out=self.b_bc[:].rearrange("p g i -> p (g i)"),
                                            in_=dr["a_b_s"].partition_broadcast(128)),
                  writes=["b_bc"], dma_sem="cst", group=grp)
        onesf = self.ev[:, 0, 0:128]
        P.add(POOL, lambda e: e.memset(onesf, 1.0), writes=[("ev", 0)])
        P.add(POOL, lambda e: e.affine_select(out=self.ident[:], in_=onesf, pattern=[[-1, 128]],
                                              compare_op=ALU.is_equal, fill=0.0, base=0, channel_multiplier=1),
              reads=[("ev", 0)], writes=["ident"])
        P.add(DVE, lambda e: e.memset(self.ones_bf[:], 1.0), writes=["ones"])
        P.add(DVE, lambda e: e.memset(self.small[:, 48:49], -0.5), writes=["neghalf"])
        for g, w in enumerate(POOL_WINDOWS):
            P.add(DVE, lambda e, g=g, w=w: e.memset(self.invtab[:, g, :], 1.0 / w), writes=["invtab"])
            for t in range(w - 1):
                P.add(DVE, lambda e, g=g, t=t: e.memset(self.invtab[:, g, t:t + 1], 1.0 / (t + 1)), writes=["invtab"])
        if "c_w_in" in dr:
            band = self.ev[:, 2, 0:128]
            band2 = self.ev[:, 2, 128:256]
            tmpm = self.ev[:, 2, 256:384]
            invf = self.ev[:, 2, 384:512]
            ek = [("ev", 2)]
            for g, w in enumerate(POOL_WINDOWS):
                P.add(POOL, lambda e: e.affine_select(out=band, in_=onesf, pattern=[[1, 128]], compare_op=ALU.is_ge, fill=0.0,
                                                      base=0, channel_multiplier=-1), reads=[("ev", 0)], writes=ek)
                P.add(POOL, lambda e, w=w: e.affine_select(out=band2, in_=band, pattern=[[-1, 128]], compare_op=ALU.is_ge, fill=0.0,
                                                           base=w - 1, channel_multiplier=1), reads=ek, writes=ek)
                P.add(DVE, lambda e, g=g, w=w: e.scalar_tensor_tensor(out=self.cm[:, g * 3 + 0, :], in0=band2, scalar=1.0 / w, in1=self.ident[:],
                                                                      op0=ALU.mult, op1=ALU.subtract), reads=ek + ["ident"], writes=["cm"])
                P.add(POOL, lambda e, w=w: e.affine_select(out=tmpm, in_=onesf, pattern=[[-1, 128]], compare_op=ALU.is_ge, fill=0.0,
                                                           base=-(129 - w), channel_multiplier=1), reads=[("ev", 0)] + ek, writes=ek)
                P.add(DVE, lambda e, g=g, w=w: e.tensor_scalar(out=self.cm[:, g * 3 + 1, :], in0=tmpm, scalar1=1.0 / w, scalar2=None, op0=ALU.mult),
                      reads=ek, writes=["cm"])
                P.add(DVE, lambda e, w=w: e.memset(invf, 1.0 / w), reads=ek, writes=ek)
                P.add(DVE, lambda e, g=g: e.tensor_copy(invf[:, 0:16], self.invtab[:, g, :]), reads=ek + ["invtab"], writes=ek)
                P.add(DVE, lambda e: e.tensor_tensor(out=tmpm, in0=band2, in1=invf, op=ALU.mult), reads=ek, writes=ek)
                P.add(DVE, lambda e, g=g: e.tensor_tensor(out=self.cm[:, g * 3 + 2, :], in0=tmpm, in1=self.ident[:], op=ALU.subtract),
                      reads=ek + ["ident"], writes=["cm"])
        b = self.bank()
        P.add(PE, lambda e: e.transpose(self.bap(b)[:, 0:NVEC], vst, self.ident[0:NVEC, 0:NVEC]),
              reads=[("ev", 1), "ident"], writes=[("ps", b)])
        P.add(DVE, lambda e: e.tensor_copy(self.vecs[:], self.bap(b)[:, 0:NVEC]), reads=[("ps", b)], writes=["vecs"])
        if has_a:
            P.add(DVE, lambda e: e.memset(wst[0:64, :, 64:128], 0.0), reads=iok, writes=iok)
            for q in range(4):
                b = self.bank()
                for r in range(4):
                    lg = q * 4 + r
                    P.add(PE, lambda e, b=b, r=r, lg=lg: e.transpose(self.bap(b)[:, r * 128:(r + 1) * 128], wst[:, lg, :], self.ident[:]),
                          reads=iok + ["ident"], writes=[("ps", b)])
                P.add(ACT, lambda e, b=b, q=q: e.activation(out=self.wmT[:, q * 4:(q + 1) * 4, :],
                                                            in_=self.bap(b).rearrange("p (r i) -> p r i", r=4), func=AF.Copy),
                      reads=[("ps", b)], writes=["wmT"])
        self.scr_keys = []

    def ykeys(self, c):
        if c < 4:
            return [("a", 2 * c + i, tt) for i in range(2) for tt in range(NSUB)]
        q = c - 4
        ks = [("vn", 2 * q), ("vn", 2 * q + 1)]
        if q == 0:
            ks += [("pt", b) for b in range(4)]
        elif q == 1:
            ks += [("pt", b) for b in range(4, 8)]
        elif q == 2:
            ks += [("d", i) for i in range(4)]
        return ks

    def yview(self, c):
        return self.yA[:, c, :] if c < 4 else self.yB[:, c - 4, :]

    def load_x(self, tile):
        P, dr = self.P, self.dr
        s, half = divmod(tile, SEQ // TILE)
        self.run_hook()
        if half == 0:
            P.add(DVE, lambda e: e.memset(self.qh[:], 0.0), writes=["qh"])
        t0 = half * TILE
        for tt in range(NSUB):
            src = dr["x"][s, :, t0 + tt * SUB:t0 + (tt + 1) * SUB].rearrange("(c p) t -> p c t", p=128)
            dst = self.xT[:, :, tt * SUB:(tt + 1) * SUB]
            wk = [k for c in range(NCH) for k in self.xk(c, tt)]
            P.add(SP, lambda e, dst=dst, src=src: e.dma_start(out=dst, in_=src), writes=wk, dma_sem="ld%d" % tt)
            self.after_load = self.after_load + [wk[0]]
        if self.pending_store is not None:
            f, self.pending_store = self.pending_store, None
            f()

    def store_x(self, tile, pre=None):
        P, dr = self.P, self.dr
        s, half = divmod(tile, SEQ // TILE)
        t0 = half * TILE
        self.run_hook()
        if self.final_norm:
            if pre is not None:
                for tt in range(NSUB):
                    self.rmsnorm(tt, pre[0], "y")

            def emit():
                for hh in range(2):
                    dst = dr["out"][s, hh * 512:(hh + 1) * 512, t0:t0 + TILE].rearrange("(c p) t -> p c t", p=128)
                    src = self.yA[:] if hh == 0 else self.yB[:]
                    rk = [k for c in range(hh * 4, hh * 4 + 4) for k in self.ykeys(c)]
                    P.add(SP, lambda e, dst=dst, src=src: e.dma_start(out=dst, in_=src), reads=rk,
                          writes=[("out", tile, hh)], dma_sem="st%d" % hh)
        else:
            def emit():
                for hh in range(2):
                    dst = dr["out"][s, hh * 512:(hh + 1) * 512, t0:t0 + TILE].rearrange("(c p) t -> p c t", p=128)
                    src = self.xT[:, hh * 4:(hh + 1) * 4, :]
                    rk = [("x", c, b) for c in range(hh * 4, hh * 4 + 4) for b in range(NBLK)]
                    P.add(SP, lambda e, dst=dst, src=src: e.dma_start(out=dst, in_=src), reads=rk,
                          writes=[("out", tile, hh)], dma_sem="st%d" % hh)
        if self.final_norm and tile + 1 < self.n_tiles:
            self.pending_store = emit
        else:
            emit()

    def rmsnorm(self, tt, gcol, mode="h"):
        n = self.norm_stats(tt)
        self.norm_apply(tt, n, gcol, mode)

    def norm_stats(self, tt):
        self.norm_sq(tt)
        return self.norm_mm(tt)

    def norm_sq(self, tt):
        P = self.P
        ts = slice(tt * SUB, (tt + 1) * SUB)
        for c in range(NCH):
            xin = self.xT[:, c, ts]
            xo = self.xsq[:, c, :]
            P.add(ACT, lambda e, xo=xo, xin=xin: e.activation(out=xo, in_=xin, func=AF.Square),
                  reads=self.xk(c, tt), writes=[("xsq", c)])

    def norm_mm(self, tt):
        P = self.P
        bs = self.bank()
        for c in range(NCH):
            xo = self.xsq[:, c, :]
            P.add(PE, lambda e, xo=xo, c=c: e.matmul(self.bap(bs), self.ones_bf[:], xo, start=(c == 0), stop=(c == NCH - 1)),
                  reads=[("xsq", c), "ones"], writes=[("ps", bs)])
        n = self.nrm_rr
        self.nrm_rr ^= 1
        nv = self.nrm[:, n, :]
        P.add(ACT, lambda e: e.activation(out=nv, in_=self.bap(bs), func=AF.Ln, scale=1.0 / D, bias=EPS),
              reads=[("ps", bs)], writes=[("nrm", n)])
        P.add(ACT, lambda e: e.activation(out=nv, in_=nv, func=AF.Exp, scale=-0.5),
              reads=[("nrm", n)], writes=[("nrm", n)])
        return n

    def norm_apply(self, tt, n, gcol, mode="h"):
        P = self.P
        ts = slice(tt * SUB, (tt + 1) * SUB)
        nv = self.nrm[:, n, :]
        order = (4, 5, 6, 7, 2, 3, 0, 1) if mode == "y" else range(NCH)
        for c in order:
            xin = self.xT[:, c, ts]
            gc = self.vecs[:, gcol + c:gcol + c + 1]
            if mode == "y":
                dst, wk = self.yview(c)[:, ts], self.ykeys(c)
            else:
                dst, wk = self.h[:, c, ts], [("h", c, tt)]
            P.add(DVE, lambda e, dst=dst, xin=xin, gc=gc: e.scalar_tensor_tensor(out=dst, in0=xin, scalar=gc, in1=nv,
                                                                                 op0=ALU.mult, op1=ALU.mult),
                  reads=self.xk(c, tt) + [("nrm", n), "vecs"], writes=wk)

    def outproj(self, wout, nj, wkey, next_norm=None):
        P = self.P

        def grp_mm(tt, m):
            ts = slice(tt * SUB, (tt + 1) * SUB)
            b = self.bank()
            for j in range(nj):
                P.add(PE, lambda e, b=b, j=j, m=m, ts=ts: e.matmul(self.bap(b), wout[:, j, m * 128:(m + 1) * 128],
                                                                  self.abuf[:, j, ts], start=(j == 0), stop=(j == nj - 1)),
                      reads=[wkey, ("a", j, tt)], writes=[("ps", b)])
            return b

        def grp_upd(tt, m, b):
            xv = self.xT[:, m, tt * SUB:(tt + 1) * SUB]
            P.add(DVE, lambda e, b=b, xv=xv: e.tensor_tensor(out=xv, in0=self.bap(b), in1=xv, op=ALU.add),
                  reads=[("ps", b)] + self.xk(m, tt), writes=self.xk(m, tt))

        def grp(tt, m):
            grp_upd(tt, m, grp_mm(tt, m))

        for m in range(NCH):
            grp(0, m)
        if next_norm is None:
            for m in range(NCH):
                grp(1, m)
            return
        gcol, mode = next_norm
        for m in range(2):
            grp(1, m)
        n0 = self.norm_stats(0)
        banks = [grp_mm(1, m) for m in range(2, NCH)]
        self.norm_apply(0, n0, gcol, mode)
        for i, m in enumerate(range(2, NCH)):
            grp_upd(1, m, banks[i])
        if mode == "y":
            self.rmsnorm(1, gcol, mode)
        else:
            self.norm_sq(1)
            self.hook = lambda: self.norm_apply(1, self.norm_mm(1), gcol, mode)

    def wblk_F(self, l, j0, nj):
        dr = self.dr
        ncol = nj * 128
        size = 2 * NCH * ncol + nj * D
        blk, over = self.walloc(size)
        wg = self.wview(blk, 0, NCH * ncol).rearrange("p (k n) -> p k n", k=NCH)
        wu = self.wview(blk, NCH * ncol, NCH * ncol).rearrange("p (k n) -> p k n", k=NCH)
        wd = self.wview(blk, 2 * NCH * ncol, nj * D).rearrange("p (j n) -> p j n", j=nj)
        c0, c1 = j0 * 128, (j0 + nj) * 128
        ks = self.wdma(blk, over, [
            (wg, dr["f_w_gate"][l, :, c0:c1].rearrange("(k p) n -> p k n", p=128), "g"),
            (wu, dr["f_w_up"][l, :, c0:c1].rearrange("(k p) n -> p k n", p=128), "u"),
            (wd, dr["f_w_down"][l, c0:c1, :].rearrange("(j p) n -> p j n", p=128), "d"),
        ])
        return dict(blk=blk, wg=wg, wu=wu, wd=wd, ks=ks)

    def ffn_part(self, l, pi, nj, wb, pre=None, nxt=None):
        P = self.P
        ks = wb["ks"]
        wg, wu, wd = wb["wg"], wb["wu"], wb["wd"]
        self.pre_norm(pre)
        for tt in range(NSUB):
            ts = slice(tt * SUB, (tt + 1) * SUB)
            for j in range(nj):
                bg = self.bank()
                bu = self.bank()
                for k in range(NCH):
                    P.add(PE, lambda e, bg=bg, k=k, j=j, ts=ts: e.matmul(self.bap(bg), wg[:, k, j * 128:(j + 1) * 128], self.h[:, k, ts],
                                                                        start=(k == 0), stop=(k == NCH - 1)),
                          reads=[ks["g"], ("h", k, tt)], writes=[("ps", bg)])
                for k in range(NCH):
                    P.add(PE, lambda e, bu=bu, k=k, j=j, ts=ts: e.matmul(self.bap(bu), wu[:, k, j * 128:(j + 1) * 128], self.h[:, k, ts],
                                                                        start=(k == 0), stop=(k == NCH - 1)),
                          reads=[ks["u"], ("h", k, tt)], writes=[("ps", bu)])
                s = self.evslot()
                sg = self.ev[:, s, :]
                P.add(ACT, lambda e, sg=sg, bg=bg: e.activation(out=sg, in_=self.bap(bg), func=AF.Silu),
                      reads=[("ps", bg)], writes=[("ev", s)])
                av = self.abuf[:, j, ts]
                P.add(DVE, lambda e, av=av, bu=bu, sg=sg: e.tensor_tensor(out=av, in0=self.bap(bu), in1=sg, op=ALU.mult),
                      reads=[("ps", bu), ("ev", s)], writes=[("a", j, tt)])
                if tt == 0 and j == 1:
                    self.run_hook()
        self.outproj(wd, nj, ks["d"], nxt)

    def wblk_A(self, j):
        dr = self.dr
        size = NCH * 2 * D + NCH * D
        blk, over = self.walloc(size)
        win = self.wview(blk, 0, NCH * 2 * D).rearrange("p (k n) -> p k n", k=NCH)
        wout = self.wview(blk, NCH * 2 * D, NCH * D).rearrange("p (k n) -> p k n", k=NCH)
        ks = self.wdma(blk, over, [
            (win[:, :, D:2 * D], dr["a_w_in"][j, :, D:2 * D].rearrange("(k p) n -> p k n", p=128), "v"),
            (win[:, :, 0:D], dr["a_w_in"][j, :, 0:D].rearrange("(k p) n -> p k n", p=128), "u"),
            (wout, dr["a_w_out"][j].rearrange("(k p) n -> p k n", p=128), "o"),
        ])
        return dict(blk=blk, win=win, wout=wout, ks=ks)

    def mixer_A(self, l, j, wb, pre=None, nxt=None):
        P = self.P
        ks = wb["ks"]
        win, wout = wb["win"], wb["wout"]
        sm = self.small
        self.pre_norm(pre)
        vn = self.scrB[:].rearrange("p (b n) -> p b n", b=NBLK)
        vraw = self.scrF[:, 0:2048].rearrange("p (s n) -> p s n", s=2)
        scr = self.scr_keys
        def v_norm(blk):
            vs = blk % 2
            mean = sm[:, 24 + 2 * blk:25 + 2 * blk]
            rs = sm[:, 40 + blk:41 + blk]
            P.add(DVE, lambda e, blk=blk, vs=vs, mean=mean, rs=rs: e.tensor_scalar(out=vn[:, blk, :], in0=vraw[:, vs, :], scalar1=mean, scalar2=rs,
                                                                               op0=ALU.subtract, op1=ALU.mult),
                  reads=[("vraw", vs, 0), ("vraw", vs, 1), ("mv", blk), ("rs", blk)], writes=[("vn", blk)])

        for blk in range(NBLK):
            tt = blk // 4
            tks = slice(blk * 128, (blk + 1) * 128)
            vs = blk % 2
            for hh in range(2):
                b = self.bank()
                for k in range(NCH):
                    P.add(PE, lambda e, b=b, k=k, hh=hh, tks=tks: e.matmul(self.bap(b), self.h[:, k, tks],
                                                                          win[:, k, D + hh * 512:D + (hh + 1) * 512],
                                                                          start=(k == 0), stop=(k == NCH - 1)),
                          reads=[ks["v"], ("h", k, tt)], writes=[("ps", b)])
                vo = vraw[:, vs, hh * 512:(hh + 1) * 512]
                P.add(ACT, lambda e, vo=vo, b=b: e.activation(out=vo, in_=self.bap(b), func=AF.Gelu),
                      reads=[("ps", b)], writes=[("vraw", vs, hh)])
                st = sm[:, vs * 12 + hh * 6:vs * 12 + (hh + 1) * 6]
                P.add(DVE, lambda e, st=st, vo=vo: e.bn_stats(st, vo), reads=[("vraw", vs, hh)], writes=[("st", vs, hh)])
            mv = sm[:, 24 + 2 * blk:26 + 2 * blk]
            P.add(DVE, lambda e, mv=mv, vs=vs: e.bn_aggr(mv, sm[:, vs * 12:vs * 12 + 12]), reads=[("st", vs, 0), ("st", vs, 1)], writes=[("mv", blk)])
            rs = sm[:, 40 + blk:41 + blk]
            P.add(POOL, lambda e, rs=rs, blk=blk: e.tensor_scalar(out=rs, in0=sm[:, 25 + 2 * blk:26 + 2 * blk], scalar1=EPS, scalar2=1.0,
                                                                op0=ALU.add, op1=ALU.mult),
                  reads=[("mv", blk)], writes=[("rs", blk)])
            P.add(POOL, lambda e, rs=rs: e.tensor_tensor(out=rs, in0=rs, in1=sm[:, 48:49], op=ALU.pow), reads=[("rs", blk), "neghalf"], writes=[("rs", blk)])
            if blk >= 1:
                v_norm(blk - 1)
            if blk == 1:
                self.run_hook()
        v_norm(NBLK - 1)
        for tt in range(NSUB):
            ts = slice(tt * SUB, (tt + 1) * SUB)
            for g in range(NCH):
                bu = self.bank()
                for k in range(NCH):
                    P.add(PE, lambda e, bu=bu, k=k, g=g, ts=ts: e.matmul(self.bap(bu), win[:, k, g * 128:(g + 1) * 128], self.h[:, k, ts],
                                                                        start=(k == 0), stop=(k == NCH - 1)),
                          reads=[ks["u"], ("h", k, tt)], writes=[("ps", bu)])
                s1 = self.evslot()
                ub = self.ev[:, s1, :]
                P.add(ACT, lambda e, ub=ub, bu=bu: e.activation(out=ub, in_=self.bap(bu), func=AF.Gelu),
                      reads=[("ps", bu)], writes=[("ev", s1)])
                bsp = self.bank()
                for r in range(4):
                    blk = tt * 4 + r
                    P.add(PE, lambda e, bsp=bsp, r=r, blk=blk, g=g: e.matmul(self.bap(bsp)[:, r * 128:(r + 1) * 128],
                                                                            vn[:, blk, g * 128:(g + 1) * 128],
                                                                            self.wmT[:, j * 8 + g, :], start=True, stop=True),
                          reads=[("vn", blk), "wmT"], writes=[("ps", bsp)])
                s2 = self.evslot()
                tb = self.ev[:, s2, :]
                gam = self.vecs[:, R_AVG + j * 8 + g:R_AVG + j * 8 + g + 1]
                bb = self.b_bc[:, j * 8 + g, :].unsqueeze(1).to_broadcast([128, 4, 128])
                P.add(DVE, lambda e, tb=tb, bsp=bsp, gam=gam, bb=bb: e.scalar_tensor_tensor(
                    out=tb.rearrange("p (r i) -> p r i", r=4), in0=self.bap(bsp).rearrange("p (r i) -> p r i", r=4),
                    scalar=gam, in1=bb, op0=ALU.mult, op1=ALU.add),
                      reads=[("ps", bsp), "vecs", "b_bc"], writes=[("ev", s2)])
                av = self.abuf[:, g, ts]
                P.add(DVE, lambda e, av=av, tb=tb, ub=ub: e.tensor_tensor(out=av, in0=tb, in1=ub, op=ALU.mult),
                      reads=[("ev", s1), ("ev", s2)], writes=[("a", g, tt)])
        self.outproj(wout, NCH, ks["o"], nxt)

    def wblk_B(self, part):
        dr = self.dr
        nc4 = 4 * 128
        size = 3 * NCH * nc4 + 4 * D
        blk, over = self.walloc(size)
        ws = []
        pieces = []
        for i in range(3):
            v = self.wview(blk, i * NCH * nc4, NCH * nc4).rearrange("p (k n) -> p k n", k=NCH)
            ws.append(v)
            c0 = i * D + part * nc4
            pieces.append((v, dr["b_w_in"][0, :, c0:c0 + nc4].rearrange("(k p) n -> p k n", p=128), "i%d" % i))
        wout = self.wview(blk, 3 * NCH * nc4, 4 * D).rearrange("p (j n) -> p j n", j=4)
        pieces.append((wout, dr["b_w_out"][0, part * nc4:(part + 1) * nc4, :].rearrange("(j p) n -> p j n", p=128), "o"))
        ks = self.wdma(blk, over, pieces)
        return dict(blk=blk, wb_=ws[0], wc=ws[1], wx=ws[2], wout=wout, ks=ks)

    def mixer_B(self, l, part, wb, tile, pre=None, nxt=None):
        P = self.P
        ks = wb["ks"]
        scr = self.scr_keys
        self.pre_norm(pre)
        qb = self.scrF[:, 0:2 * 514].rearrange("p (s n) -> p s n", s=2)
        yb = self.scrF[:, 1028:1028 + 1024].rearrange("p (s n) -> p s n", s=2)
        rr = 0
        for tt in range(NSUB):
            ts = slice(tt * SUB, (tt + 1) * SUB)
            for ci in range(4):
                c = part * 4 + ci
                banks = []
                for wi, w in enumerate((wb["wb_"], wb["wc"], wb["wx"])):
                    wkey = ks["i%d" % wi]
                    b = self.bank()
                    banks.append(b)
                    for k in range(NCH):
                        P.add(PE, lambda e, b=b, k=k, w=w, ci=ci, ts=ts: e.matmul(self.bap(b), w[:, k, ci * 128:(ci + 1) * 128], self.h[:, k, ts],
                                                                                 start=(k == 0), stop=(k == NCH - 1)),
                              reads=[wkey, ("h", k, tt)], writes=[("ps", b)])
                bgb, bgc, bxt = banks
                s = self.evslot()
                xs = self.ev[:, s, :]
                P.add(ACT, lambda e, xs=xs, bxt=bxt: e.activation(out=xs, in_=self.bap(bxt), func=AF.Copy),
                      reads=[("ps", bxt)], writes=[("ev", s)])
                qs = rr
                rr ^= 1
                q = qb[:, qs, :]
                yv = yb[:, qs, :]
                hal = self.qh[:, c, :]
                P.add(DVE, lambda e, q=q, hal=hal: e.tensor_copy(q[:, 0:2], hal), reads=["qh"], writes=[("q", qs)] + scr)
                P.add(DVE, lambda e, q=q, bgc=bgc, xs=xs: e.tensor_tensor(out=q[:, 2:514], in0=self.bap(bgc), in1=xs, op=ALU.mult),
                      reads=[("ps", bgc), ("ev", s)], writes=[("q", qs)])
                P.add(DVE, lambda e, q=q, hal=hal: e.tensor_copy(hal, q[:, 512:514]), reads=[("q", qs)], writes=["qh"])
                w0 = self.vecs[:, R_CONV + 0 * 8 + c:R_CONV + 0 * 8 + c + 1]
                w1 = self.vecs[:, R_CONV + 1 * 8 + c:R_CONV + 1 * 8 + c + 1]
                w2 = self.vecs[:, R_CONV + 2 * 8 + c:R_CONV + 2 * 8 + c + 1]
                P.add(DVE, lambda e, yv=yv, q=q, w0=w0: e.tensor_scalar(out=yv, in0=q[:, 0:512], scalar1=w0, scalar2=None, op0=ALU.mult),
                      reads=[("q", qs), "vecs"], writes=[("y", qs)] + scr)
                P.add(DVE, lambda e, yv=yv, q=q, w1=w1: e.scalar_tensor_tensor(out=yv, in0=q[:, 1:513], scalar=w1, in1=yv,
                                                                              op0=ALU.mult, op1=ALU.add),
                      reads=[("q", qs), ("y", qs), "vecs"], writes=[("y", qs)])
                P.add(DVE, lambda e, yv=yv, q=q, w2=w2: e.scalar_tensor_tensor(out=yv, in0=q[:, 2:514], scalar=w2, in1=yv,
                                                                              op0=ALU.mult, op1=ALU.add),
                      reads=[("q", qs), ("y", qs), "vecs"], writes=[("y", qs)])
                av = self.abuf[:, ci, ts]
                P.add(DVE, lambda e, av=av, bgb=bgb, yv=yv: e.tensor_tensor(out=av, in0=self.bap(bgb), in1=yv, op=ALU.mult),
                      reads=[("ps", bgb), ("y", qs)], writes=[("a", ci, tt)])
                if tt == 0 and ci == 1:
                    self.run_hook()
        self.outproj(wb["wout"], 4, ks["o"], nxt)

    def wblk_C(self, part):
        dr = self.dr
        nc4 = 4 * 128
        size = NCH * nc4 + 4 * 256 + 4 * D
        blk, over = self.walloc(size)
        win = self.wview(blk, 0, NCH * nc4).rearrange("p (k n) -> p k n", k=NCH)
        wgr = self.wview(blk, NCH * nc4, 4 * 256).rearrange("p (g n) -> p g n", g=4)
        wout = self.wview(blk, NCH * nc4 + 4 * 256, 4 * D).rearrange("p (j n) -> p j n", j=4)
        ks = self.wdma(blk, over, [
            (win, dr["c_w_in"][0, :, part * nc4:(part + 1) * nc4].rearrange("(k p) n -> p k n", p=128), "i"),
            (wgr, dr["c_w_grp"][0, part * 2:part * 2 + 2].rearrange("g (b p) d -> p (g b) d", p=128), "g"),
            (wout, dr["c_w_out"][0, part * nc4:(part + 1) * nc4, :].rearrange("(j p) n -> p j n", p=128), "o"),
        ])
        return dict(blk=blk, win=win, wgr=wgr, wout=wout, ks=ks)

    def mixer_C(self, l, part, wb, tile, pre=None, nxt=None):
        P = self.P
        ks = wb["ks"]
        first = (tile % (SEQ // TILE) == 0)
        self.pre_norm(pre)
        ptok = self.scrB[:, 0:NBLK * 512].rearrange("p (b n) -> p b n", b=NBLK)
        db = self.scrB[:, NBLK * 512:NBLK * 512 + 4 * 512].rearrange("p (s n) -> p s n", s=4)
        win, wgr = wb["win"], wb["wgr"]
        for tt in range(NSUB):
            ts = slice(tt * SUB, (tt + 1) * SUB)
            for r in range(4):
                blk = tt * 4 + r
                tks = slice(blk * 128, (blk + 1) * 128)
                b = self.bank()
                for k in range(NCH):
                    P.add(PE, lambda e, b=b, k=k, tks=tks: e.matmul(self.bap(b), self.h[:, k, tks], win[:, k, :],
                                                                   start=(k == 0), stop=(k == NCH - 1)),
                          reads=[ks["i"], ("h", k, tt)], writes=[("ps", b)])
                pv = ptok[:, blk, :]
                if r % 2 == 0:
                    P.add(ACT, lambda e, pv=pv, b=b: e.activation(out=pv, in_=self.bap(b), func=AF.Copy),
                          reads=[("ps", b)], writes=[("pt", blk)])
                else:
                    P.add(DVE, lambda e, pv=pv, b=b: e.tensor_copy(pv, self.bap(b)), reads=[("ps", b)], writes=[("pt", blk)])
                if blk == 1:
                    self.run_hook()
            for ci in range(4):
                c = part * 4 + ci
                gi = ci // 2
                g = part * 2 + gi
                cs = slice(ci * 128, (ci + 1) * 128)
                b2 = self.bank()
                for r in range(4):
                    blk = tt * 4 + r
                    fb = first and blk == 0
                    m0 = self.cm[:, g * 3 + (2 if fb else 0), :]
                    o = self.bap(b2)[:, r * 128:(r + 1) * 128]
                    P.add(PE, lambda e, o=o, blk=blk, cs=cs, m0=m0, fb=fb: e.matmul(o, ptok[:, blk, cs], m0, start=True, stop=fb),
                          reads=[("pt", blk), "cm"], writes=[("ps", b2)])
                    if not fb:
                        if blk > 0:
                            pl, pk = ptok[:, blk - 1, cs], ("pt", blk - 1)
                        else:
                            pl, pk = self.cstash[:, part, cs], ("cst", part)
                        m1 = self.cm[:, g * 3 + 1, :]
                        P.add(PE, lambda e, o=o, pl=pl, m1=m1: e.matmul(o, pl, m1, start=False, stop=True),
                              reads=[pk, "cm"], writes=[("ps", b2)])
                dv = db[:, ci, :]
                P.add(DVE, lambda e, dv=dv, b2=b2: e.tensor_copy(dv, self.bap(b2)), reads=[("ps", b2)], writes=[("d", ci)])
                if ci % 2 == 1:
                    for aq in range(2):
                        co = gi * 2 + aq
                        b3 = self.bank()
                        for bq in range(2):
                            P.add(PE, lambda e, b3=b3, bq=bq, aq=aq, gi=gi: e.matmul(self.bap(b3), wgr[:, gi * 2 + bq, aq * 128:(aq + 1) * 128],
                                                                                    db[:, gi * 2 + bq, :], start=(bq == 0), stop=(bq == 1)),
                                  reads=[ks["g"], ("d", gi * 2 + bq)], writes=[("ps", b3)])
                        av = self.abuf[:, co, ts]
                        sc = self.vecs[:, R_CSC + part * 4 + co:R_CSC + part * 4 + co + 1]
                        P.add(ACT, lambda e, av=av, b3=b3, sc=sc: e.activation(out=av, in_=self.bap(b3), func=AF.Copy, scale=sc),
                              reads=[("ps", b3), "vecs"], writes=[("a", co, tt)])
        P.add(ACT, lambda e: e.activation(out=self.cstash[:, part, :], in_=ptok[:, NBLK - 1, :], func=AF.Copy),
              reads=[("pt", NBLK - 1)], writes=[("cst", part)])
        self.outproj(wb["wout"], 4, ks["o"], nxt)


_CACHE = {}


def _get_prog(layers, final_norm):
    key = (tuple(layers), final_norm)
    if key not in _CACHE:
        mk = MK(layers, final_norm)
        nc = mk.build()
        _CACHE[key] = (nc, mk.input_names)
    return _CACHE[key]


def _pack_vecs(inp):
    rows = [inp["norm_mix_g"].reshape(32, 128), inp["norm_ffn_g"].reshape(32, 128),
            inp["final_norm_g"].reshape(8, 128), inp["a_v_norm_g"].reshape(16, 128),
            inp["b_conv_w"].reshape(24, 128), inp["c_scale"].reshape(8, 128)]
    return np.ascontiguousarray(np.concatenate(rows, axis=0).astype(np.float32))


def _run(x, inp, layers, final_norm, n_cores=N_CORES):
    nc, names = _get_prog(layers, final_norm)
    shared = dict(inp)
    shared["vecs"] = _pack_vecs(inp)
    shared["a_b_s"] = np.ascontiguousarray(inp["a_b_s"].reshape(1, -1))
    in_maps = []
    for c in range(n_cores):
        m = {k: np.ascontiguousarray(shared[k]) for k in names if k != "x"}
        m["x"] = np.ascontiguousarray(x[c * SPC:(c + 1) * SPC].transpose(0, 2, 1))
        in_maps.append(m)
    res = run_bass_kernel_spmd(nc, in_maps, core_ids=list(range(n_cores)))
    return np.concatenate([np.ascontiguousarray(r["out"].transpose(0, 2, 1)) for r in res.results], axis=0)


def kernel(**inputs):
    inp = {k: np.asarray(v) for k, v in inputs.items()}
    x = np.ascontiguousarray(inp["x"], dtype=np.float32)
    out = _run(x, inp, list(range(DEPTH)), True)
    return out.astype(np.float32)
```
